# Optimizing a Trainium2 kernel written in Bass

```python
import jax
import jax.numpy as jnp
from jax import lax
import numpy as np

D_MODEL = 1024
BATCH = 2
SEQ = 8192
DEPTH = 1

CHUNK = 64
Q_BLOCK = 128
EPS = 1e-6
ROPE_THETA = 10000.0

MLA_HEADS = 8
MLA_Q_RANK = 384
MLA_KV_RANK = 256
MLA_NOPE = 64
MLA_ROPE = 32
MLA_V = 64

GLA_HEADS = 4
GLA_DK = 64
GLA_DV = 128
GLA_GATE_RANK = 16
GLA_TAU = 16.0

N_BRANCH = 2

PEER_HEADS = 8
PEER_NKEYS = 128
PEER_HALF = 128
PEER_TOPK = 16
PEER_BLOCK = 128
PEER_EXPERTS = PEER_NKEYS * PEER_NKEYS

IN_SIZES = (MLA_Q_RANK, MLA_KV_RANK, MLA_ROPE, GLA_HEADS * GLA_DK, GLA_HEADS * GLA_DK, GLA_HEADS * GLA_DV, GLA_GATE_RANK, GLA_HEADS * GLA_DV, N_BRANCH * D_MODEL)
IN_WIDTH = sum(IN_SIZES)

kernel_name = 'hybrid_mla_gla_peer_block'


def _split_points():
    return [int(s) for s in np.cumsum(IN_SIZES)[:-1]]


def rms_norm(x, g):
    xf = x.astype(jnp.float32)
    y = xf * lax.rsqrt(jnp.mean(xf * xf, axis=-1, keepdims=True) + EPS)
    return (y * g.astype(jnp.float32)).astype(x.dtype)


def apply_rope(x, positions):
    half = x.shape[-1] // 2
    inv_freq = ROPE_THETA ** (-jnp.arange(half, dtype=jnp.float32) / half)
    ang = positions.astype(jnp.float32)[:, :, None, None] * inv_freq
    cos, sin = jnp.cos(ang), jnp.sin(ang)
    xf = x.astype(jnp.float32)
    x1, x2 = xf[..., :half], xf[..., half:]
    return jnp.concatenate([x1 * cos - x2 * sin, x2 * cos + x1 * sin], axis=-1).astype(x.dtype)


def mla_attention(q_lat, kv_lat, k_rope, positions, g_q, w_qb, g_kv, w_kvb):
    B, S, _ = q_lat.shape
    H = MLA_HEADS
    dqk = MLA_NOPE + MLA_ROPE
    q = (rms_norm(q_lat, g_q) @ w_qb).reshape(B, S, H, dqk)
    q = jnp.concatenate([q[..., :MLA_NOPE], apply_rope(q[..., MLA_NOPE:], positions)], axis=-1)
    kv = (rms_norm(kv_lat, g_kv) @ w_kvb).reshape(B, S, H, MLA_NOPE + MLA_V)
    k_nope, v = kv[..., :MLA_NOPE], kv[..., MLA_NOPE:]
    k_pe = apply_rope(k_rope[:, :, None, :], positions)
    k = jnp.concatenate([k_nope, jnp.broadcast_to(k_pe, (B, S, H, MLA_ROPE))], axis=-1)
    scale = dqk ** -0.5
    nb = S // Q_BLOCK
    qb = q.reshape(B, nb, Q_BLOCK, H, dqk).transpose(1, 0, 2, 3, 4)
    key_chunk = jnp.arange(S) // CHUNK

    def block(args):
        qi, i = args
        q_chunk = (i * Q_BLOCK + jnp.arange(Q_BLOCK)) // CHUNK
        s = jnp.einsum('bqhd,bkhd->bhqk', qi, k, preferred_element_type=jnp.float32) * scale
        mask = key_chunk[None, :] <= q_chunk[:, None]
        s = jnp.where(mask[None, None], s, -1e30)
        p = jax.nn.softmax(s, axis=-1)
        return jnp.einsum('bhqk,bkhd->bqhd', p.astype(v.dtype), v)

    out = lax.map(block, (qb, jnp.arange(nb)))
    return out.transpose(1, 0, 2, 3, 4).reshape(B, S, H * MLA_V)


def gla_mixer(q, k, v, gate_lr, out_gate, w_a2, b_a2, g_gn):
    B, S, _ = q.shape
    H, dk, dv = GLA_HEADS, GLA_DK, GLA_DV
    nc = S // CHUNK
    f32 = jnp.float32
    qc = q.astype(f32).reshape(B, nc, CHUNK, H, dk) * (dk ** -0.5)
    kc = k.astype(f32).reshape(B, nc, CHUNK, H, dk)
    vc = v.astype(f32).reshape(B, nc, CHUNK, H, dv)
    log_a = jax.nn.log_sigmoid((gate_lr @ w_a2 + b_a2).astype(f32)) / GLA_TAU
    log_a = log_a.reshape(B, nc, CHUNK, H, dk)
    cum = jnp.cumsum(log_a, axis=2)
    cum_last = cum[:, :, -1]
    k_dec = kc * jnp.exp(cum_last[:, :, None] - cum)
    d_state = jnp.einsum('bclhk,bclhv->bchkv', k_dec, vc)
    chunk_decay = jnp.exp(cum_last)

    def step(state, inp):
        dec, ds = inp
        new = dec[..., None] * state + ds
        return new, new

    s0 = jnp.zeros((B, H, dk, dv), f32)
    _, states = lax.scan(step, s0, (chunk_decay.transpose(1, 0, 2, 3), d_state.transpose(1, 0, 2, 3, 4)))
    o = jnp.einsum('bclhk,cbhkv->bclhv', qc, states)
    o = o * lax.rsqrt(jnp.mean(o * o, axis=-1, keepdims=True) + EPS) * g_gn.astype(f32)
    o = o.reshape(B, S, H * dv) * jax.nn.silu(out_gate.astype(f32))
    return o.astype(q.dtype)


def peer_ffn(h, w_q, sub_keys, u_tab, v_tab):
    B, S, D = h.shape
    T = B * S
    PH, K = PEER_HEADS, PEER_TOPK
    hf = h.reshape(T, D)
    q = (hf @ w_q).reshape(T, PH, 2, PEER_HALF)
    scores = jnp.einsum('thpd,hpnd->thpn', q, sub_keys, preferred_element_type=jnp.float32)
    s, idx = lax.top_k(scores, K)
    cand = (s[:, :, 0, :, None] + s[:, :, 1, None, :]).reshape(T, PH, K * K)
    cand_idx = (idx[:, :, 0, :, None] * PEER_NKEYS + idx[:, :, 1, None, :]).reshape(T, PH, K * K)
    best, pos = lax.top_k(cand, K)
    expert = jnp.take_along_axis(cand_idx, pos, axis=-1)
    gate = jax.nn.softmax(best, axis=-1)
    nb = T // PEER_BLOCK

    def block(args):
        xb, eb, gb = args
        u = jnp.take(u_tab, eb, axis=0)
        a = jax.nn.gelu(jnp.einsum('td,thkd->thk', xb, u, preferred_element_type=jnp.float32), approximate=False)
        c = (a * gb).astype(h.dtype)
        return jnp.einsum('thk,thkd->td', c, jnp.take(v_tab, eb, axis=0))

    y = lax.map(block, (hf.reshape(nb, PEER_BLOCK, D), expert.reshape(nb, PEER_BLOCK, PH, K), gate.reshape(nb, PEER_BLOCK, PH, K)))
    return y.reshape(B, S, D)


def setup_inputs(seed: int = 0) -> dict:
    key = jax.random.key(seed)
    ks = jax.random.split(key, 24)
    L, D = DEPTH, D_MODEL
    f32 = jnp.float32

    def nrm(k, shape, fan_in):
        return jax.random.normal(k, shape, f32) * (fan_in ** -0.5)

    def gain(k, shape):
        return 1.0 + 0.01 * jax.random.normal(k, shape, f32)

    x = jax.random.normal(ks[0], (BATCH, SEQ, D), f32)
    offset = jax.random.randint(ks[1], (BATCH, 1), 0, 4096, dtype=jnp.int32)
    positions = offset + jnp.arange(SEQ, dtype=jnp.int32)[None, :]
    return {
        'x': x,
        'positions': positions,
        'g_mix': gain(ks[2], (L, D)),
        'w_in': nrm(ks[3], (L, D, IN_WIDTH), D),
        'g_q_lat': gain(ks[4], (L, MLA_Q_RANK)),
        'w_qb': nrm(ks[5], (L, MLA_Q_RANK, MLA_HEADS * (MLA_NOPE + MLA_ROPE)), MLA_Q_RANK),
        'g_kv_lat': gain(ks[6], (L, MLA_KV_RANK)),
        'w_kvb': nrm(ks[7], (L, MLA_KV_RANK, MLA_HEADS * (MLA_NOPE + MLA_V)), MLA_KV_RANK),
        'w_a2': nrm(ks[8], (L, GLA_GATE_RANK, GLA_HEADS * GLA_DK), GLA_GATE_RANK),
        'b_a2': 0.1 * jax.random.normal(ks[9], (L, GLA_HEADS * GLA_DK), f32),
        'g_gla': gain(ks[10], (L, GLA_HEADS, GLA_DV)),
        'w_branch_a': nrm(ks[11], (L, MLA_HEADS * MLA_V, D), MLA_HEADS * MLA_V),
        'w_branch_b': nrm(ks[12], (L, GLA_HEADS * GLA_DV, D), GLA_HEADS * GLA_DV),
        'w_out': nrm(ks[13], (L, D, D), D),
        'g_ffn': gain(ks[14], (L, D)),
        'w_peer_q': nrm(ks[15], (L, D, PEER_HEADS * 2 * PEER_HALF), D),
        'peer_sub_keys': nrm(ks[16], (L, PEER_HEADS, 2, PEER_NKEYS, PEER_HALF), PEER_HALF),
        'peer_u': nrm(ks[17], (L, PEER_EXPERTS, D), D),
        'peer_v': nrm(ks[18], (L, PEER_EXPERTS, D), PEER_HEADS * PEER_TOPK),
        'g_final': gain(ks[19], (D,)),
    }


def reference(x, positions, g_mix, w_in, g_q_lat, w_qb, g_kv_lat, w_kvb, w_a2, b_a2, g_gla, w_branch_a, w_branch_b, w_out, g_ffn, w_peer_q, peer_sub_keys, peer_u, peer_v, g_final):
    B, S, D = x.shape
    pts = _split_points()
    for l in range(DEPTH):
        h = rms_norm(x, g_mix[l])
        proj = h @ w_in[l]
        q_lat, kv_lat, k_rope, gq, gk, gv, g_lr, g_out, br = jnp.split(proj, pts, axis=-1)
        y_a = mla_attention(q_lat, kv_lat, k_rope, positions, g_q_lat[l], w_qb[l], g_kv_lat[l], w_kvb[l])
        y_b = gla_mixer(gq, gk, gv, g_lr, g_out, w_a2[l], b_a2[l], g_gla[l])
        gates = jax.nn.sigmoid(br.astype(jnp.float32)).reshape(B, S, N_BRANCH, D)
        merged = gates[:, :, 0] * (y_a @ w_branch_a[l]) + gates[:, :, 1] * (y_b @ w_branch_b[l])
        x = x + merged.astype(x.dtype) @ w_out[l]
        x = x + peer_ffn(rms_norm(x, g_ffn[l]), w_peer_q[l], peer_sub_keys[l], peer_u[l], peer_v[l])
    return rms_norm(x, g_final)
```

```python
import math
from contextlib import ExitStack

import numpy as np
import concourse.bass as bass
import concourse.mybir as mybir
from concourse.bass_utils import run_bass_kernel_spmd

F32 = mybir.dt.float32
BF16 = mybir.dt.bfloat16
I32 = mybir.dt.int32
U32 = mybir.dt.uint32
AF = mybir.ActivationFunctionType
ALU = mybir.AluOpType
AX = mybir.AxisListType

SEM_LIMIT = 30000
EPS = 1e-6
NT_ALL = 64
NT_OWN = 16
PI = math.pi


class Sched:
    def __init__(self, nc, es, n_dma_sems=16):
        self.nc = nc
        self.es = es
        self.eng = {"pe": nc.tensor, "dve": nc.vector, "act": nc.scalar,
                    "pool": nc.gpsimd, "sp": nc.sync}
        self.nsem = 0
        self.sem = {}
        self.cnt = {}
        self.all_sems = []
        for e in ("pe", "dve", "act", "pool"):
            self._new_eng_sem(e)
        self.seen = {e: {} for e in self.eng}
        self.dsem = {"sp": [], "pool": []}
        self.dcnt = {"sp": [], "pool": []}
        self.drr = {"sp": 0, "pool": 0}
        for q, n in (("sp", n_dma_sems), ("pool", 8)):
            for i in range(n):
                self.dsem[q].append(self._alloc_sem())
                self.dcnt[q].append(0)
        self.bg = []
        self.res = {}
        self.excl = set()
        self.n_inst = 0
        self.n_wait = 0

    def _alloc_sem(self):
        s = self.es.enter_context(self.nc.semaphore("s%d" % self.nsem))
        self.nsem += 1
        return s

    def _new_eng_sem(self, e):
        if e in self.sem:
            self.all_sems.append((self.sem[e], self.cnt[e]))
        self.sem[e] = self._alloc_sem()
        self.cnt[e] = 0

    def _wait(self, e, tok):
        if tok is None:
            return
        s, v = tok
        if v <= 0:
            return
        k = id(s)
        d = self.seen[e]
        if d.get(k, 0) >= v:
            return
        self.eng[e].wait_ge(s, v)
        self.n_wait += 1
        d[k] = v

    @staticmethod
    def _k(x):
        return x if isinstance(x, (str, tuple, int)) else id(x)

    def _deps(self, e, reads, writes):
        toks = []
        own = self.sem.get(e)
        for r in reads:
            st = self.res.get(r)
            if st is None:
                continue
            if st["w"] is not None:
                toks.append(st["w"])
            if r in self.excl:
                for t in st["r"].values():
                    if not (t[0] is own):
                        toks.append(t)
        for w in writes:
            st = self.res.get(w)
            if st is None:
                continue
            if st["w"] is not None and not (e == "pe" and st["w"][0] is own):
                toks.append(st["w"])
            for t in st["r"].values():
                toks.append(t)
        for t in toks:
            self._wait(e, t)

    def _update(self, tok, reads, writes):
        for r in reads:
            st = self.res.setdefault(r, {"w": None, "r": {}})
            st["r"][id(tok[0])] = tok
        for w in writes:
            self.res[w] = {"w": tok, "r": {}}

    def op(self, e, fn, reads=(), writes=()):
        reads = [self._k(x) for x in reads]
        writes = [self._k(x) for x in writes]
        if self.cnt[e] >= SEM_LIMIT:
            self._new_eng_sem(e)
        self._deps(e, reads, writes)
        ins = fn(self.eng[e])
        self.cnt[e] += 1
        ins.then_inc(self.sem[e], 1)
        tok = (self.sem[e], self.cnt[e])
        self._update(tok, reads, writes)
        self.n_inst += 1
        return tok

    def dma(self, q, out, in_, reads=(), writes=(), indirect=None, slow=False):
        reads = [self._k(x) for x in reads]
        writes = [self._k(x) for x in writes]
        dsem, dcnt = self.dsem[q], self.dcnt[q]
        i = self.drr[q]
        self.drr[q] = (i + 1) % len(dsem)
        if dcnt[i] >= SEM_LIMIT:
            self._wait(q, (dsem[i], dcnt[i]))
            self.all_sems.append((dsem[i], dcnt[i]))
            dsem[i] = self._alloc_sem()
            dcnt[i] = 0
        s = dsem[i]
        self._wait(q, (s, dcnt[i]))
        self._deps(q, reads, writes)
        if indirect is not None:
            ins = self.eng[q].indirect_dma_start(out=out, out_offset=None, in_=in_,
                                                 in_offset=indirect)
        else:
            ins = self.eng[q].dma_start(out=out, in_=in_, allow_slow_non_contiguous=True) if slow else self.eng[q].dma_start(out=out, in_=in_)
        ins.then_inc(s, 16)
        dcnt[i] += 16
        tok = (s, dcnt[i])
        self._update(tok, reads, writes)
        self.n_inst += 1
        return tok

    def dma_bg(self, out, in_):
        s = self._alloc_sem()
        self.eng["pool"].dma_start(out=out, in_=in_).then_inc(s, 16)
        self.bg.append((s, 16))
        self.n_inst += 1

    def wait_bg(self):
        for e in self.eng:
            for t in self.bg:
                self._wait(e, t)

    def barrier(self):
        toks = [(self.sem[e], self.cnt[e]) for e in self.sem]
        for q in self.dsem:
            toks += [(s, c) for s, c in zip(self.dsem[q], self.dcnt[q])]
        for e in self.eng:
            for t in toks:
                self._wait(e, t)
        self.res = {}


class Pool:
    def __init__(self, nc, es, name, shape, dtype, bufs, psum=False):
        self.t = []
        for i in range(bufs):
            if psum:
                t = es.enter_context(nc.psum_tensor("%s%d" % (name, i), shape, dtype))
            else:
                t = es.enter_context(nc.sbuf_tensor("%s%d" % (name, i), shape, dtype))
            self.t.append(t)
        self.i = 0

    def get(self):
        t = self.t[self.i]
        self.i = (self.i + 1) % len(self.t)
        return t


C_QL = (0, 384)
C_KV = (384, 672)
C_GQ = (672, 928)
C_GKVL = (928, 1712)
C_GO_BR = (1712, 4272)


def build_program(phases=("p0", "p1", "p2", "p3a", "p3b"), n_all=NT_ALL, n_own=NT_OWN, dbg=False, stop=99):
    nc = bass.Bass("TRN2", target_bir_lowering=False)

    def din(name, shape, dt=F32):
        return nc.dram_tensor(name, list(shape), dt, kind="ExternalInput").ap()

    xb = din("xb", [8192, 1024])
    xo = din("xo", [2048, 1024])
    posb = din("posb", [128, 64], I32)
    poso = din("poso", [128, 16], I32)
    cidx = din("cidx", [128, 32], U32)
    masks = din("masks", [128, 4, 128])
    c_ident = din("c_ident", [128, 128])
    c_rem = din("c_rem", [128, 128])
    c_cind = din("c_cind", [128, 2])
    c_cind16 = din("c_cind16", [128, 2])
    c_iota = din("c_iota", [128, 16])
    c_invf = din("c_invf", [128, 16])
    g_mix = din("g_mix", [1024])
    w_in = din("w_in", [1024, 4272])
    g_q_lat = din("g_q_lat", [384])
    w_qb = din("w_qb", [384, 768])
    g_kv_lat = din("g_kv_lat", [256])
    w_kvb = din("w_kvb", [256, 1024])
    w_a2 = din("w_a2", [16, 256])
    b_a2 = din("b_a2", [256])
    g_gla = din("g_gla", [512])
    w_ba = din("w_branch_a", [512, 1024])
    w_bb = din("w_branch_b", [512, 1024])
    w_out = din("w_out", [1024, 1024])
    g_ffn = din("g_ffn", [1024])
    w_pq = din("w_peer_q", [1024, 2048])
    subk = din("peer_sub_keys", [16, 128, 128])
    peer_u = din("peer_u", [16384, 1024])
    peer_v = din("peer_v", [16384, 1024])
    g_final = din("g_final", [1024])
    out = nc.dram_tensor("out", [2048, 1024], F32, kind="ExternalOutput").ap()
    states = nc.dram_tensor("states", [128 * 128, 256], F32, kind="Internal").ap()
    x1s = nc.dram_tensor("x1s", [2048, 1024], F32, kind="Internal").ap()
    yaTs = nc.dram_tensor("yaTs", [NT_OWN * 128, 512], BF16, kind="Internal").ap()
    uvb = nc.dram_tensor("peer_uv_bf", [16384, 2048], BF16, kind="Internal").ap()
    dbg_out = {}
    if dbg:
        dbg_out["d_states"] = nc.dram_tensor("d_states", [128 * 128, 256], F32, kind="ExternalOutput").ap()
        dbg_out["d_ya"] = nc.dram_tensor("d_ya", [2048, 512], F32, kind="ExternalOutput").ap()
        dbg_out["d_x1"] = nc.dram_tensor("d_x1", [2048, 1024], F32, kind="ExternalOutput").ap()
        dbg_out["d_yb"] = nc.dram_tensor("d_yb", [2048, 512], F32, kind="ExternalOutput").ap()

    with ExitStack() as es:
        S = Sched(nc, es)

        def sb(name, shape, dt, st=es):
            return st.enter_context(nc.sbuf_tensor(name, shape, dt))

        def pst(name, shape, dt, st):
            return st.enter_context(nc.psum_tensor(name, shape, dt))

        S.excl.update(["T", "B", "C", "G", ("D", 0), ("D", 1), "A", "T2", "QL", ("Q", 0), ("Q", 1), "G1", "GO0", "GO1",
                       ("BR", 0), ("BR", 1), ("BR", 2), ("BR", 3), "QP", ("SC", 0), ("SC", 1), ("Y", 0), ("Y", 1)])

        def ACT(o, i, func, r, w, **kw):
            S.op("act", lambda e: e.activation(out=o, in_=i, func=func, **kw), r, w)

        def MM(o, lhsT, rhs, start, stop, r, w):
            S.op("pe", lambda e: e.matmul(o, lhsT=lhsT, rhs=rhs, start=start, stop=stop), r, w)

        def TT(eng, o, a, b, op, r, w):
            S.op(eng, lambda e: e.tensor_tensor(out=o, in0=a, in1=b, op=op), r, w)

        def TS(eng, o, a, s1, s2, op0, op1, r, w):
            if op1 is None:
                S.op(eng, lambda e: e.tensor_scalar(out=o, in0=a, scalar1=s1, scalar2=None, op0=op0), r, w)
            else:
                S.op(eng, lambda e: e.tensor_scalar(out=o, in0=a, scalar1=s1, scalar2=s2, op0=op0, op1=op1), r, w)

        def STT(o, a, sc, b, op0, op1, r, w):
            S.op("dve", lambda e: e.scalar_tensor_tensor(out=o, in0=a, scalar=sc, in1=b, op0=op0, op1=op1), r, w)

        def CP(eng, o, i, r, w):
            if eng == "act":
                S.op("act", lambda e: e.copy(out=o, in_=i), r, w)
            else:
                S.op(eng, lambda e: e.tensor_copy(out=o, in_=i), r, w)

        identf = sb("identf", [128, 128], F32)
        identb = sb("identb", [128, 128], BF16)
        remf = sb("remf", [128, 128], F32)
        cindf = sb("cindf", [128, 2], F32)
        cind16 = sb("cind16", [128, 2], F32)
        iota16 = sb("iota16", [128, 16], F32)
        invf = sb("invf", [128, 16], F32)
        gmix = sb("gmix", [128, 8], F32)
        S.dma("sp", identf[:], c_ident[:, :], writes=[identf])
        S.dma("sp", remf[:], c_rem[:, :], writes=[remf])
        S.dma("sp", cindf[:], c_cind[:, :], writes=[cindf])
        S.dma("sp", cind16[:], c_cind16[:, :], writes=[cind16])
        S.dma("sp", iota16[:], c_iota[:, :], writes=[iota16])
        S.dma("sp", invf[:], c_invf[:, :], writes=[invf])
        S.dma("sp", gmix[:], g_mix.rearrange("(k p) -> p k", p=128), writes=[gmix], slow=True)
        CP("dve", identb[:], identf[:], [identf], [identb])

        if "p3b" in phases:
            for c in range(16):
                S.dma_bg(uvb[c * 1024:(c + 1) * 1024, 0:1024], peer_u[c * 1024:(c + 1) * 1024, :])
            for c in range(16):
                S.dma_bg(uvb[c * 1024:(c + 1) * 1024, 1024:2048], peer_v[c * 1024:(c + 1) * 1024, :])

        def TR(o, i, r, w):
            S.op("pe", lambda e: e.transpose(out=o, in_=i, identity=identb[:]), list(r) + [identb], w)

        cvt_rr = [0, 0]

        def load_w(st, dst, wd, c0, c1, nk, gain=None, dst_off=0):
            cvt_rr[1] += 1
            stage = Pool(nc, st, "wstage%d_" % cvt_rr[1], [128, 1024], F32, 3)
            n = c1 - c0
            for k in range(nk):
                for s0 in range(0, n, 1024):
                    m = min(1024, n - s0)
                    sg = stage.get()
                    S.dma("sp", sg[:, 0:m], wd[k * 128:(k + 1) * 128, c0 + s0:c0 + s0 + m], writes=[sg])
                    o = dst[:, k, dst_off + s0:dst_off + s0 + m]
                    eng = ("dve", "act")[cvt_rr[0] % 2]
                    cvt_rr[0] += 1
                    if gain is not None:
                        if eng == "dve":
                            TS(eng, o, sg[:, 0:m], gain[:, k:k + 1], None, ALU.mult, None, [sg, gain], [dst])
                        else:
                            ACT(o, sg[:, 0:m], AF.Copy, [sg, gain], [dst], scale=gain[:, k:k + 1])
                    else:
                        CP(eng, o, sg[:, 0:m], [sg], [dst])

        def rstd_from_ss(rs, ss, n, cols=1):
            ACT(rs[:], ss[:], AF.Ln, [ss, eps_t], [rs], scale=1.0 / n, bias=eps_t[:, 0:1])
            ACT(rs[:], rs[:], AF.Exp, [rs], [rs], scale=-0.5)

        eps_t = sb("eps_t", [128, 1], F32)
        one_t = sb("one_t", [128, 1], F32)
        S.op("dve", lambda e: e.memset(eps_t[:], EPS), [], [eps_t])
        S.op("dve", lambda e: e.memset(one_t[:], 1.0), [], [one_t])

        class Front:
            def __init__(self, st, Tps, tkey, nb=2):
                cvt_rr[1] += 1
                u = "f%d_" % cvt_rr[1]
                self.xt = Pool(nc, st, u + "xt", [128, 1024], F32, nb)
                self.hb = Pool(nc, st, u + "hb", [128, 1024], BF16, nb)
                self.hT = Pool(nc, st, u + "hT", [128, 8, 128], BF16, nb)
                self.ss = Pool(nc, st, u + "ss", [128, 1], F32, 2)
                self.rs = Pool(nc, st, u + "rs", [128, 1], F32, 2)
                self.Tps = Tps
                self.tkey = tkey

            def __call__(self, src):
                xt, hb, hT, ss, rs = self.xt.get(), self.hb.get(), self.hT.get(), self.ss.get(), self.rs.get()
                S.dma("sp", xt[:], src, writes=[xt])
                ACT(hb[:], xt[:], AF.Square, [xt], [hb, ss], accum_out=ss[:])
                rstd_from_ss(rs, ss, 1024)
                ACT(hb[:], xt[:], AF.Copy, [xt, rs], [hb], scale=rs[:])
                T = self.Tps
                for k in range(8):
                    TR(T[:, k * 128:(k + 1) * 128], hb[:, k * 128:(k + 1) * 128], [hb], [self.tkey])
                CP("dve", hT[:], T[:, 0:1024].rearrange("p (k t) -> p k t", k=8), [self.tkey], [hT])
                return xt, hT

        def trig_tables(cos_t, sin_t, tmp, pos_d, n, scale, name):
            pi_ = sb(name + "_pi", [128, n], I32, tmp)
            pf = sb(name + "_pf", [128, n], F32, tmp)
            ang = sb(name + "_ang", [128, n, 16], F32, tmp)
            ki = sb(name + "_ki", [128, n, 16], I32, tmp)
            kf = sb(name + "_kf", [128, n, 16], F32, tmp)
            r0 = sb(name + "_r0", [128, n, 16], F32, tmp)
            y = sb(name + "_y", [128, n, 16], F32, tmp)
            m = sb(name + "_m", [128, n, 16], F32, tmp)
            S.dma("sp", pi_[:], pos_d, writes=[pi_])
            CP("dve", pf[:], pi_[:], [pi_], [pf])
            TT("dve", ang[:], pf[:].unsqueeze(2).to_broadcast([128, n, 16]),
               invf[:].unsqueeze(1).to_broadcast([128, n, 16]), ALU.mult, [pf, invf], [ang])
            TS("dve", ki[:], ang[:], 1.0 / (2 * PI), None, ALU.mult, None, [ang], [ki])
            CP("dve", kf[:], ki[:], [ki], [kf])
            c1 = 6.28125
            c2 = float(np.float32(2 * PI - c1))
            c3 = float(2 * PI - c1 - c2)
            STT(r0[:], kf[:], -c1, ang[:], ALU.mult, ALU.add, [kf, ang], [r0])
            STT(r0[:], kf[:], -c2, r0[:], ALU.mult, ALU.add, [kf, r0], [r0])
            STT(r0[:], kf[:], -c3, r0[:], ALU.mult, ALU.add, [kf, r0], [r0])
            for shift, dst in ((0.0, sin_t), (PI / 2, cos_t)):
                TS("dve", y[:], r0[:], shift, None, ALU.add, None, [r0], [y])
                for _ in range(2):
                    TS("dve", m[:], y[:], PI, -2 * PI, ALU.is_gt, ALU.mult, [y], [m])
                    TT("dve", y[:], y[:], m[:], ALU.add, [y, m], [y])
                    TS("dve", m[:], y[:], -PI, 2 * PI, ALU.is_lt, ALU.mult, [y], [m])
                    TT("dve", y[:], y[:], m[:], ALU.add, [y, m], [y])
                TS("dve", y[:], y[:], PI, -PI, ALU.min, ALU.max, [y], [y])
                ACT(dst[:], y[:], AF.Sin, [y], [dst])
                if scale != 1.0:
                    TS("dve", dst[:], dst[:], scale, None, ALU.mult, None, [dst], [dst])

        if "p0" in phases:
            with ExitStack() as p0:
                W1b = sb("W1b", [128, 8, 784], BF16, p0)
                wa2b = sb("wa2b", [32, 256], BF16, p0)
                if True:
                    tmp = p0
                    load_w(tmp, W1b, w_in, C_GKVL[0], C_GKVL[1], 8, gmix)
                    wa2s = sb("wa2s", [32, 256], F32, tmp)
                    S.dma("sp", wa2s[0:16, :], w_a2[:, :], writes=[wa2s])
                    S.dma("sp", wa2s[16:17, :], b_a2.unsqueeze(0), writes=[wa2s])
                    CP("dve", wa2b[0:17, :], wa2s[0:17, :], [wa2s], [wa2b])
                    S.barrier()
                T = pst("p0T", [128, 1024], BF16, p0)
                Bp = pst("p0B", [128, 512], F32, p0)
                Cp = pst("p0C", [128, 512], F32, p0)
                Gp = pst("p0G", [128, 512], F32, p0)
                Dp = pst("p0D", [128, 1024], F32, p0)
                front = Front(p0, T, "T")
                glrT = Pool(nc, p0, "glrT", [32, 128], BF16, 2)
                for t in glrT.t:
                    S.op("pool", lambda e: e.memset(t[:], 1.0), [], [t])
                e1p = Pool(nc, p0, "e1", [128, 256], F32, 2)
                lap = Pool(nc, p0, "la", [128, 256], F32, 2)
                dkp = Pool(nc, p0, "dk", [128, 256], F32, 2)
                decp = Pool(nc, p0, "dec", [128, 4], F32, 2)
                kdp = Pool(nc, p0, "kd", [128, 2, 256], BF16, 2)
                gvp = Pool(nc, p0, "gv", [128, 512], BF16, 2)
                stp = Pool(nc, p0, "st", [128, 2, 128], F32, 4)
                old = stp.get()
                S.op("dve", lambda e: e.memset(old[:], 0.0), [], [old])
                st_old = [old]
                pend_b = [None]
                nxt = front(xb[0:128, :]) if n_all > 0 else None
                for i in range(n_all):
                    if stop == 1:
                        break
                    xt, hT = nxt
                    if stop == 2:
                        break
                    for k in range(8):
                        MM(Bp[:, :], hT[:, k, :], W1b[:, k, 0:512], k == 0, k == 7, [hT, W1b], ["B"])
                    for k in range(8):
                        MM(Cp[:, 0:256], hT[:, k, :], W1b[:, k, 512:768], k == 0, k == 7, [hT, W1b], ["C"])
                    for k in range(8):
                        MM(Gp[0:16, 0:128], W1b[:, k, 768:784], hT[:, k, :], k == 0, k == 7, [hT, W1b], ["G"])
                    if i + 1 < n_all:
                        nxt = front(xb[(i + 1) * 128:(i + 2) * 128, :])
                    g1 = glrT.get()
                    CP("act", g1[0:16, :], Gp[0:16, 0:128], ["G"], [g1])
                    MM(Cp[:, 256:512], g1[0:17, :], wa2b[0:17, :], True, True, [g1, wa2b], ["C"])
                    if pend_b[0] is not None:
                        pend_b[0]()
                        pend_b[0] = None
                    if stop == 3:
                        break
                    e1, la, dk, dec, kd, gv = e1p.get(), lap.get(), dkp.get(), decp.get(), kdp.get(), gvp.get()
                    ACT(e1[:], Cp[:, 256:512], AF.Exp, ["C"], [e1], scale=-1.0)
                    ACT(la[:], e1[:], AF.Ln, [e1], [la], bias=one_t[:, 0:1])
                    if stop == 31:
                        break
                    MM(Gp[:, 128:384], remf[:], la[:], True, True, [remf, la], ["G"])
                    for p in range(2):
                        MM(Gp[:, 384 + 2 * p:386 + 2 * p], la[:, p * 128:(p + 1) * 128], cind16[:], True, True,
                           [la, cind16], ["G"])
                    if stop == 32:
                        break
                    ACT(dk[:], Gp[:, 128:384], AF.Exp, ["G"], [dk], scale=-1.0)
                    if stop == 331:
                        break
                    ACT(dec[:], Gp[:, 384:388], AF.Exp, ["G"], [dec], scale=-1.0)
                    if stop == 332:
                        break
                    if stop == 33:
                        break
                    for c in range(2):
                        STT(kd[:, c, :], dk[:], cindf[:, c:c + 1], Bp[:, 0:256], ALU.mult, ALU.mult,
                            [dk, cindf, "B"], [kd])
                    if stop == 34:
                        break
                    CP("act", gv[:, 0:256], Bp[:, 256:512], ["B"], [gv])
                    CP("act", gv[:, 256:512], Cp[:, 0:256], ["C"], [gv])
                    def part_b(i=i, kd=kd, gv=gv, dec=dec):
                        for c in range(2):
                            for p in range(2):
                                MM(Dp[:, (c * 2 + p) * 256:(c * 2 + p + 1) * 256], kd[:, c, p * 128:(p + 1) * 128],
                                   gv[:, p * 256:(p + 1) * 256], True, True, [kd, gv], [("D", c)])
                        for c in range(2):
                            new = stp.get()
                            old = st_old[0]
                            for p in range(2):
                                for q in range(2):
                                    rows = slice(q * 64, (q + 1) * 64)
                                    c0 = (c * 2 + p) * 256 + q * 128
                                    STT(new[rows, p, :], old[rows, p, :], dec[rows, p * 2 + c:p * 2 + c + 1],
                                        Dp[rows, c0:c0 + 128], ALU.mult, ALU.add, [old, dec, ("D", c)], [new])
                            ch = 2 * i + c
                            S.dma("sp", states[ch * 128:(ch + 1) * 128, :].rearrange("p (a b) -> p a b", a=2),
                                  new[:], reads=[new], writes=[("states", ch)])
                            if dbg:
                                S.dma("sp", dbg_out["d_states"][ch * 128:(ch + 1) * 128, :].rearrange("p (a b) -> p a b", a=2),
                                      new[:], reads=[new], writes=[("dstates", ch)])
                            st_old[0] = new
                    pend_b[0] = part_b
                if pend_b[0] is not None:
                    pend_b[0]()
                S.barrier()

        if "p1" in phases or "p2" in phases:
            with ExitStack() as kv:
                cosq = sb("tq_cos", [128, NT_OWN, 16], F32, kv)
                sinq = sb("tq_sin", [128, NT_OWN, 16], F32, kv)
                with ExitStack() as tmp:
                    trig_tables(cosq, sinq, tmp, poso[:, :], NT_OWN, 96 ** -0.5, "tq")
                    S.barrier()
                KnT = sb("KnT", [128, 4, 8192], BF16, kv)
                KpeT = sb("KpeT", [128, 8192], BF16, kv)
                V = sb("V", [128, NT_ALL, 8, 66], BF16, kv)
                with ExitStack() as p1:
                    W1a = sb("W1a", [128, 8, 288], BF16, p1)
                    Wkn = sb("Wkn", [128, 2, 512], BF16, p1)
                    Wkv = sb("Wkv", [128, 2, 512], BF16, p1)
                    gkv = sb("gkv", [128, 2], F32, p1)
                    cosb = sb("tb_cos", [128, NT_ALL, 16], F32, p1)
                    sinb = sb("tb_sin", [128, NT_ALL, 16], F32, p1)
                    with ExitStack() as tmp:
                        trig_tables(cosb, sinb, tmp, posb[:, :], NT_ALL, 1.0, "tb")
                        S.barrier()
                    S.dma("sp", gkv[:], g_kv_lat.rearrange("(k p) -> p k", p=128), writes=[gkv], slow=True)
                    if stop > 10:
                        S.op("pool", lambda e: e.memset(V[:], 1.0), [], ["Vinit"])
                    with ExitStack() as tmp:
                        if stop > 11:
                            load_w(tmp, W1a, w_in, C_KV[0], C_KV[1], 8, gmix)
                        kvs = sb("kvs", [128, 2, 8, 2, 64], F32, tmp)
                        for k in range(2 if stop > 11 else 0):
                            S.dma("sp", kvs[:, k], w_kvb[k * 128:(k + 1) * 128, :].rearrange("p (h t d) -> p h t d", h=8, t=2),
                                  writes=[kvs])
                        for k in range(2 if stop > 11 else 0):
                            TS("dve", Wkn[:, k, :].rearrange("p (h d) -> p h d", h=8), kvs[:, k, :, 0, :], gkv[:, k:k + 1], None,
                               ALU.mult, None, [kvs, gkv], [Wkn])
                            TS("pool", Wkv[:, k, :].rearrange("p (h d) -> p h d", h=8), kvs[:, k, :, 1, :], gkv[:, k:k + 1], None,
                               ALU.mult, None, [kvs, gkv], [Wkv])
                        S.barrier()
                    T = pst("p1T", [128, 1024], BF16, p1)
                    Ap = pst("p1A", [128, 512], F32, p1)
                    T2 = pst("p1T2", [128, 1024], BF16, p1)
                    KVp = Pool(nc, p1, "p1KV", [128, 512], F32, 2, psum=True)
                    S.excl.update(id(t) for t in KVp.t)
                    front = Front(p1, T, "T")
                    jk = sb("p1jk", [128, 256], BF16, p1)
                    ssp = Pool(nc, p1, "p1ss", [128, 1], F32, 2)
                    rsp = Pool(nc, p1, "p1rs", [128, 1], F32, 2)
                    kvnp = Pool(nc, p1, "kvn", [128, 256], BF16, 2)
                    kvnTp = Pool(nc, p1, "kvnT", [128, 2, 128], BF16, 2)
                    tp = Pool(nc, p1, "p1t", [128, 4, 16], F32, 2)
                    kpp = Pool(nc, p1, "kp", [128, 32], F32, 2)
                    kp4p = Pool(nc, p1, "kp4", [128, 4, 32], BF16, 2)
                    n_p1 = n_all if ("p1" in phases and stop > 12) else 0
                    nxt = front(xb[0:128, :]) if n_p1 > 0 else None
                    for i in range(n_p1):
                        xt, hT = nxt
                        for k in range(8):
                            MM(Ap[:, 0:288], hT[:, k, :], W1a[:, k, :], k == 0, k == 7, [hT, W1a], ["A"])
                        if i + 1 < n_p1:
                            nxt = front(xb[(i + 1) * 128:(i + 2) * 128, :])
                        ss, rs, kvn, kvnT = ssp.get(), rsp.get(), kvnp.get(), kvnTp.get()
                        ACT(jk[:], Ap[:, 0:256], AF.Square, ["A"], [jk, ss], accum_out=ss[:])
                        rstd_from_ss(rs, ss, 256)
                        ACT(kvn[:], Ap[:, 0:256], AF.Copy, ["A", rs], [kvn], scale=rs[:])
                        for k in range(2):
                            TR(T2[:, k * 128:(k + 1) * 128], kvn[:, k * 128:(k + 1) * 128], [kvn], ["T2"])
                        CP("dve", kvnT[:], T2[:, 0:256].rearrange("p (k t) -> p k t", k=2), ["T2"], [kvnT])
                        if stop == 13:
                            break
                        for r_ in range(2):
                            KV = KVp.get()
                            for pp in range(2):
                                pr = 2 * r_ + pp
                                for k in range(2):
                                    MM(KV[:, pp * 128:(pp + 1) * 128], Wkn[:, k, pr * 128:(pr + 1) * 128], kvnT[:, k, :],
                                       k == 0, k == 1, [Wkn, kvnT], [KV])
                            for k in range(2):
                                MM(KV[:, 256:512], kvnT[:, k, :], Wkv[:, k, r_ * 256:(r_ + 1) * 256], k == 0, k == 1,
                                   [Wkv, kvnT], [KV])
                            import os as _os
                            _sk = _os.environ.get("KSKIP", "")
                            if "b" not in _sk:
                                CP("act", KnT[:, 2 * r_:2 * r_ + 2, i * 128:(i + 1) * 128],
                                   KV[:, 0:256].rearrange("p (a t) -> p a t", a=2), [KV], [("KnT", i, r_)])
                            if "c" not in _sk:
                                CP("dve", V[:, i, 4 * r_:4 * r_ + 4, 0:64],
                                   KV[:, 256:512].rearrange("p (h d) -> p h d", h=4), [KV, "Vinit"], [("V", i, r_)])
                        if stop == 14:
                            break
                        t_, kp, kp4 = tp.get(), kpp.get(), kp4p.get()
                        x1_, x2_ = Ap[:, 256:272], Ap[:, 272:288]
                        c_, s_ = cosb[:, i, :], sinb[:, i, :]
                        TT("dve", t_[:, 0, :], x1_, c_, ALU.mult, ["A", cosb], [t_])
                        TT("dve", t_[:, 1, :], x2_, s_, ALU.mult, ["A", sinb], [t_])
                        TT("dve", t_[:, 2, :], x2_, c_, ALU.mult, ["A", cosb], [t_])
                        TT("dve", t_[:, 3, :], x1_, s_, ALU.mult, ["A", sinb], [t_])
                        TT("dve", kp[:, 0:16], t_[:, 0, :], t_[:, 1, :], ALU.subtract, [t_], [kp])
                        TT("dve", kp[:, 16:32], t_[:, 2, :], t_[:, 3, :], ALU.add, [t_], [kp])
                        CP("dve", kp4[:], kp[:].unsqueeze(1).to_broadcast([128, 4, 32]), [kp], [kp4])
                        TR(T2[:, 256:384], kp4[:].rearrange("p a d -> p (a d)"), [kp4], ["T2"])
                        CP("act", KpeT[:, i * 128:(i + 1) * 128], T2[:, 256:384], ["T2"], [("KpeT", i)])
                    S.barrier()

                if "p2" in phases:
                    with ExitStack() as p2:
                        Wq = sb("Wq", [128, 8, 384], BF16, p2)
                        Wqn = sb("Wqn", [128, 3, 512], BF16, p2)
                        Wqr = sb("Wqr", [128, 3, 256], BF16, p2)
                        gq = sb("gq", [128, 3], F32, p2)
                        maskf = sb("maskf", [128, 4, 128], F32, p2)
                        S.dma("sp", gq[:], g_q_lat.rearrange("(k p) -> p k", p=128), writes=[gq], slow=True)
                        S.dma("sp", maskf[:], masks[:, :, :], writes=[maskf])
                        scale = 96 ** -0.5
                        with ExitStack() as tmp:
                            load_w(tmp, Wq, w_in, C_QL[0], C_QL[1], 8, gmix)
                            qs = sb("qs", [128, 3, 8, 96], F32, tmp)
                            for k in range(3):
                                S.dma("sp", qs[:, k], w_qb[k * 128:(k + 1) * 128, :].rearrange("p (h d) -> p h d", h=8),
                                      writes=[qs])
                            for k in range(3):
                                TS("dve", Wqn[:, k, :].rearrange("p (h d) -> p h d", h=8), qs[:, k, :, 0:64], gq[:, k:k + 1], None,
                                   ALU.mult, None, [qs, gq], [Wqn])
                                TS("pool", Wqr[:, k, :].rearrange("p (h d) -> p h d", h=8), qs[:, k, :, 64:96], gq[:, k:k + 1], None,
                                   ALU.mult, None, [qs, gq], [Wqr])
                            S.barrier()
                        T = pst("p2T", [128, 1024], BF16, p2)
                        QL = pst("p2QL", [128, 512], F32, p2)
                        Qp = pst("p2Q", [128, 1024], F32, p2)
                        Sps = Pool(nc, p2, "p2S", [128, 512], F32, 2, psum=True)
                        S.excl.update(id(t) for t in Sps.t)
                        Ops = Pool(nc, p2, "p2O", [128, 512], F32, 2, psum=True)
                        S.excl.update(id(t) for t in Ops.t)
                        front = Front(p2, T, "T", nb=1)
                        jk = sb("p2jk", [128, 384], BF16, p2)
                        ssp = Pool(nc, p2, "p2ss", [128, 1], F32, 2)
                        rsp = Pool(nc, p2, "p2rs", [128, 1], F32, 2)
                        qlnp = Pool(nc, p2, "qln", [128, 384], BF16, 1)
                        qlnTp = Pool(nc, p2, "qlnT", [128, 3, 128], BF16, 1)
                        qzp = Pool(nc, p2, "qz", [128, 16, 128], BF16, 1)
                        for t in qzp.t:
                            S.op("pool", lambda e: e.memset(t[:], 0.0), [], [t])
                        qrp = Pool(nc, p2, "qr", [128, 8, 32], BF16, 2)
                        t4p = Pool(nc, p2, "p2t", [128, 4, 8, 16], F32, 1)
                        qTp = Pool(nc, p2, "qT", [128, 16, 128], BF16, 2)
                        PTp = Pool(nc, p2, "PT", [128, 512], BF16, 3)
                        PTf = Pool(nc, p2, "PTf", [128, 512], F32, 1)
                        recp = Pool(nc, p2, "rec", [128, 1], F32, 4)
                        yap = Pool(nc, p2, "ya", [128, 512], BF16, 2)
                        yaTp = Pool(nc, p2, "yaTt", [128, 512], BF16, 2)
                        yafp = Pool(nc, p2, "yaf", [128, 512 if dbg else 2], F32, 1)
                        def qpath(j):
                            xt, hT = front(xo[j * 128:(j + 1) * 128, :])
                            for k in range(8):
                                MM(QL[:, 0:384], hT[:, k, :], Wq[:, k, :], k == 0, k == 7, [hT, Wq], ["QL"])
                            ss, rs, qln, qlnT = ssp.get(), rsp.get(), qlnp.get(), qlnTp.get()
                            ACT(jk[:], QL[:, 0:384], AF.Square, ["QL"], [jk, ss], accum_out=ss[:])
                            rstd_from_ss(rs, ss, 384)
                            ACT(qln[:], QL[:, 0:384], AF.Copy, ["QL", rs], [qln], scale=rs[:])
                            for k in range(3):
                                TR(T[:, k * 128:(k + 1) * 128], qln[:, k * 128:(k + 1) * 128], [qln], ["T"])
                            CP("dve", qlnT[:], T[:, 0:384].rearrange("p (k t) -> p k t", k=3), ["T"], [qlnT])
                            for k in range(3):
                                MM(Qp[:, 0:512], qlnT[:, k, :], Wqn[:, k, :], k == 0, k == 2, [qlnT, Wqn], [("Q", 0)])
                            for k in range(3):
                                MM(Qp[:, 512:768], qlnT[:, k, :], Wqr[:, k, :], k == 0, k == 2, [qlnT, Wqr], [("Q", 1)])
                            qz, qr, t4, qT = qzp.get(), qrp.get(), t4p.get(), qTp.get()
                            qzf = qz[:].rearrange("p m d -> p (m d)")
                            for e_ in range(2):
                                ACT(qzf[:, 0:1024].rearrange("p (a x) -> p a x", x=256)[:, :, e_ * 192:e_ * 192 + 64],
                                    Qp[:, 0:512].rearrange("p (a e d) -> p a e d", e=2, d=64)[:, :, e_, :],
                                    AF.Copy, [("Q", 0)], [qz], scale=scale)
                            Qr3 = Qp[:, 512:768].rearrange("p (h d) -> p h d", h=8)
                            X1, X2 = Qr3[:, :, 0:16], Qr3[:, :, 16:32]
                            Cq = cosq[:, j, :].unsqueeze(1).to_broadcast([128, 8, 16])
                            Sq = sinq[:, j, :].unsqueeze(1).to_broadcast([128, 8, 16])
                            TT("dve", t4[:, 0], X1, Cq, ALU.mult, [("Q", 1), cosq], [t4])
                            TT("dve", t4[:, 1], X2, Sq, ALU.mult, [("Q", 1), sinq], [t4])
                            TT("dve", t4[:, 2], X2, Cq, ALU.mult, [("Q", 1), cosq], [t4])
                            TT("dve", t4[:, 3], X1, Sq, ALU.mult, [("Q", 1), sinq], [t4])
                            TT("dve", qr[:, :, 0:16], t4[:, 0], t4[:, 1], ALU.subtract, [t4], [qr])
                            TT("dve", qr[:, :, 16:32], t4[:, 2], t4[:, 3], ALU.add, [t4], [qr])
                            for b_ in range(4):
                                CP("pool", qzf[:, 1024:2048].rearrange("p (a y) -> p a y", y=512)[:, :, b_ * 160:b_ * 160 + 32],
                                   qr[:].rearrange("p (a b) d -> p a b d", b=4)[:, :, b_, :], [qr], [qz])
                            for half in range(2):
                                for m in range(8):
                                    TR(T[:, m * 128:(m + 1) * 128], qz[:, half * 8 + m, :], [qz], ["T"])
                                CP("dve" if half == 0 else "act", qT[:, half * 8:half * 8 + 8, :],
                                   T[:, 0:1024].rearrange("p (m t) -> p m t", m=8), ["T"], [qT])
                            return qT

                        nxt_q = qpath(0) if n_own > 0 else None
                        for j in range(n_own):
                            qT = nxt_q
                            ya, yaf = yap.get(), yafp.get()
                            steps = [(h, g) for h in range(8) for g in range(j + 1)]

                            def emit_S(h, g):
                                Sg = Sps.get()
                                for t in range(4):
                                    kt = 4 * g + t
                                    MM(Sg[:, t * 128:(t + 1) * 128], KnT[:, h // 2, kt * 128:(kt + 1) * 128], qT[:, h, :],
                                       True, False, [qT], [Sg])
                                    MM(Sg[:, t * 128:(t + 1) * 128], KpeT[:, kt * 128:(kt + 1) * 128], qT[:, 8 + h, :],
                                       False, True, [qT], [Sg])
                                return Sg

                            Sg_next = emit_S(*steps[0])
                            Op = None
                            for si, (h, g) in enumerate(steps):
                                if si == len(steps) // 2 and j + 1 < n_own:
                                    nxt_q = qpath(j + 1)
                                Sg = Sg_next
                                if si + 1 < len(steps):
                                    Sg_next = emit_S(*steps[si + 1])
                                if g == 0:
                                    Op = Ops.get()
                                okey = Op
                                PT = PTp.get()
                                if g == j:
                                    pf = PTf.get()
                                    ACT(pf[:], Sg[:], AF.Exp, [Sg], [pf])
                                    TT("dve", PT[:], pf[:], maskf[:].rearrange("p a q -> p (a q)"), ALU.mult, [pf, maskf], [PT])
                                else:
                                    ACT(PT[:], Sg[:], AF.Exp, [Sg], [PT])
                                for t in range(4):
                                    kt = 4 * g + t
                                    MM(Op[:, 0:65], PT[:, t * 128:(t + 1) * 128], V[:, kt, h, 0:65],
                                       g == 0 and t == 0, g == j and t == 3, [PT], [okey])
                                if g == j:
                                    rec = recp.get()
                                    S.op("dve", lambda e: e.reciprocal(out=rec[:], in_=Op[:, 64:65]), [okey], [rec])
                                    TS("dve", ya[:, h * 64:(h + 1) * 64], Op[:, 0:64], rec[:, 0:1], None, ALU.mult, None,
                                       [okey, rec], [ya])
                                    if dbg:
                                        TS("dve", yaf[:, h * 64:(h + 1) * 64], Op[:, 0:64], rec[:, 0:1], None, ALU.mult, None,
                                           [okey, rec], [yaf])
                            if dbg:
                                S.dma("sp", dbg_out["d_ya"][j * 128:(j + 1) * 128, :], yaf[:], reads=[yaf], writes=[("dya", j)])
                            for k in range(4):
                                TR(T[:, k * 128:(k + 1) * 128], ya[:, k * 128:(k + 1) * 128], [ya], ["T"])
                            yaTt = yaTp.get()
                            CP("dve", yaTt[:], T[:, 0:512], ["T"], [yaTt])
                            S.dma("sp", yaTs[j * 128:(j + 1) * 128, :], yaTt[:], reads=[yaTt], writes=[("yaTs", j)])
                        S.barrier()

        if "p3a" in phases:
            with ExitStack() as p3:
                Wgq = sb("Wgq", [128, 8, 256], BF16, p3)
                Wgb = sb("Wgb", [128, 8, 2560], BF16, p3)
                WA = sb("WA", [128, 4, 1024], BF16, p3)
                WB = sb("WB", [128, 4, 1024], BF16, p3)
                WO = sb("WO", [128, 8, 1024], BF16, p3)
                ggla = sb("ggla", [128, 512], F32, p3)
                cix = sb("cix", [128, 32], U32, p3)
                S.dma("sp", ggla[:], g_gla.unsqueeze(0).to_broadcast([128, 512]), writes=[ggla])
                S.dma("sp", cix[:], cidx[:, :], writes=[cix])
                with ExitStack() as tmp:
                    load_w(tmp, Wgq, w_in, C_GQ[0], C_GQ[1], 8, gmix)
                    load_w(tmp, Wgb, w_in, C_GO_BR[0], C_GO_BR[1], 8, gmix)
                    load_w(tmp, WA, w_ba, 0, 1024, 4)
                    load_w(tmp, WB, w_bb, 0, 1024, 4)
                    load_w(tmp, WO, w_out, 0, 1024, 8)
                    S.barrier()
                T = pst("p3T", [128, 1024], BF16, p3)
                G1 = pst("p3G1", [128, 512], F32, p3)
                GO = pst("p3GO", [128, 1024], F32, p3)
                BR = pst("p3BR", [128, 2048], F32, p3)
                front = Front(p3, T, "T")
                gqzp = Pool(nc, p3, "gqz", [128, 4, 2, 128], BF16, 2)
                for t in gqzp.t:
                    S.op("pool", lambda e: e.memset(t[:], 0.0), [], [t])
                stg = Pool(nc, p3, "stg", [128, 256], F32, 4)
                stb = Pool(nc, p3, "stb", [128, 256], BF16, 4)
                sgp = Pool(nc, p3, "sg", [128, 512], F32, 2)
                gatp = Pool(nc, p3, "gat", [128, 2048], F32, 2)
                jk = sb("p3jk", [128, 128], BF16, p3)
                ssp = Pool(nc, p3, "p3ss", [128, 4], F32, 2)
                rsp = Pool(nc, p3, "p3rs", [128, 4], F32, 2)
                ybp = Pool(nc, p3, "yb", [128, 512], BF16, 2)
                ybfp = Pool(nc, p3, "ybf", [128, 512], F32, 2)
                ybTp = Pool(nc, p3, "ybT", [128, 4, 128], BF16, 2)
                m1p = Pool(nc, p3, "m1", [128, 1024], F32, 2)
                mbp = Pool(nc, p3, "mb", [128, 1024], BF16, 2)
                mTp = Pool(nc, p3, "mT", [128, 8, 128], BF16, 2)
                x1p = Pool(nc, p3, "x1", [128, 1024], F32, 2)
                yaTp3 = Pool(nc, p3, "yaT3", [128, 4, 128], BF16, 2)
                nxt = front(xo[0:128, :]) if n_own > 0 else None
                for j in range(n_own):
                    xt, hT = nxt
                    yaT = yaTp3.get()
                    S.dma("sp", yaT[:].rearrange("p k t -> p (k t)"), yaTs[j * 128:(j + 1) * 128, :], writes=[yaT])
                    for p in range(2):
                        for k in range(8):
                            MM(G1[:, p * 128:(p + 1) * 128], Wgq[:, k, p * 128:(p + 1) * 128], hT[:, k, :], k == 0, k == 7,
                               [hT, Wgq], ["G1"])
                    for k in range(8):
                        MM(GO[:, 0:512], hT[:, k, :], Wgb[:, k, 0:512], k == 0, k == 7, [hT, Wgb], ["GO0"])
                    for n_ in range(4):
                        for k in range(8):
                            MM(BR[:, n_ * 512:(n_ + 1) * 512], hT[:, k, :], Wgb[:, k, 512 + n_ * 512:1024 + n_ * 512],
                               k == 0, k == 7, [hT, Wgb], [("BR", n_)])
                    if j + 1 < n_own:
                        nxt = front(xo[(j + 1) * 128:(j + 2) * 128, :])
                    gqz = gqzp.get()
                    for h in range(4):
                        for c in range(2):
                            rows = slice((h % 2) * 64, (h % 2) * 64 + 64)
                            TS("dve", gqz[rows, h, c, c * 64:(c + 1) * 64],
                               G1[rows, (h // 2) * 128 + c * 64:(h // 2) * 128 + (c + 1) * 64], 0.125, None, ALU.mult, None,
                               ["G1"], [gqz])
                    sbs = []
                    for c in range(2):
                        sg_, sb_ = stg.get(), stb.get()
                        S.dma("pool", sg_[:], states[:, :], reads=[cix], writes=[sg_],
                              indirect=bass.IndirectOffsetOnAxis(ap=cix[:, 2 * j + c:2 * j + c + 1], axis=0))
                        CP("dve", sb_[:], sg_[:], [sg_], [sb_])
                        sbs.append(sb_)
                    for h in range(4):
                        for c in range(2):
                            MM(GO[:, 512 + h * 128:512 + (h + 1) * 128], gqz[:, h, c, :],
                               sbs[c][:, (h // 2) * 128:(h // 2 + 1) * 128], c == 0, c == 1, [gqz, sbs[c]], ["GO1"])
                    sg, gat, ss, rs = sgp.get(), gatp.get(), ssp.get(), rsp.get()
                    ACT(sg[:], GO[:, 0:512], AF.Silu, ["GO0"], [sg])
                    TT("dve", sg[:], sg[:], ggla[:], ALU.mult, [sg, ggla], [sg])
                    for h in range(4):
                        ACT(jk[:], GO[:, 512 + h * 128:512 + (h + 1) * 128], AF.Square, ["GO1"], [jk, ss], accum_out=ss[:, h:h + 1])
                    rstd_from_ss(rs, ss, 128)
                    yb, ybf, ybT = ybp.get(), ybfp.get(), ybTp.get()
                    for h in range(4):
                        STT(yb[:, h * 128:(h + 1) * 128], GO[:, 512 + h * 128:512 + (h + 1) * 128], rs[:, h:h + 1],
                            sg[:, h * 128:(h + 1) * 128], ALU.mult, ALU.mult, ["GO1", rs, sg], [yb])
                        if dbg:
                            STT(ybf[:, h * 128:(h + 1) * 128], GO[:, 512 + h * 128:512 + (h + 1) * 128], rs[:, h:h + 1],
                                sg[:, h * 128:(h + 1) * 128], ALU.mult, ALU.mult, ["GO1", rs, sg], [ybf])
                    if dbg:
                        S.dma("sp", dbg_out["d_yb"][j * 128:(j + 1) * 128, :], ybf[:], reads=[ybf], writes=[("dyb", j)])
                    for k in range(4):
                        TR(T[:, k * 128:(k + 1) * 128], yb[:, k * 128:(k + 1) * 128], [yb], ["T"])
                    CP("dve", ybT[:], T[:, 0:512].rearrange("p (k t) -> p k t", k=4), ["T"], [ybT])
                    for n_ in range(4):
                        ACT(gat[:, n_ * 512:(n_ + 1) * 512], BR[:, n_ * 512:(n_ + 1) * 512], AF.Sigmoid, [("BR", n_)], [gat])
                    for n_ in range(2):
                        for k in range(4):
                            MM(BR[:, n_ * 512:(n_ + 1) * 512], yaT[:, k, :], WA[:, k, n_ * 512:(n_ + 1) * 512],
                               k == 0, k == 3, [WA, gat, yaT], [("BR", n_)])
                    for n_ in range(2):
                        for k in range(4):
                            MM(BR[:, 1024 + n_ * 512:1024 + (n_ + 1) * 512], ybT[:, k, :], WB[:, k, n_ * 512:(n_ + 1) * 512],
                               k == 0, k == 3, [WB, ybT, gat], [("BR", 2 + n_)])
                    m1, mb, mT, x1 = m1p.get(), mbp.get(), mTp.get(), x1p.get()
                    TT("dve", m1[:], gat[:, 0:1024], BR[:, 0:1024], ALU.mult, [gat, ("BR", 0), ("BR", 1)], [m1])
                    TT("dve", gat[:, 1024:2048], gat[:, 1024:2048], BR[:, 1024:2048], ALU.mult, [gat, ("BR", 2), ("BR", 3)], [gat])
                    TT("pool", mb[:], m1[:], gat[:, 1024:2048], ALU.add, [m1, gat], [mb])
                    for k in range(8):
                        TR(T[:, k * 128:(k + 1) * 128], mb[:, k * 128:(k + 1) * 128], [mb], ["T"])
                    CP("dve", mT[:], T[:, 0:1024].rearrange("p (k t) -> p k t", k=8), ["T"], [mT])
                    for n_ in range(2):
                        for k in range(8):
                            MM(GO[:, n_ * 512:(n_ + 1) * 512], mT[:, k, :], WO[:, k, n_ * 512:(n_ + 1) * 512], k == 0, k == 7,
                               [mT, WO], ["GO0", "GO1"])
                    TT("dve", x1[:], xt[:], GO[:, :], ALU.add, [xt, "GO0", "GO1"], [x1])
                    S.dma("sp", x1s[j * 128:(j + 1) * 128, :], x1[:], reads=[x1], writes=[("x1s", j)])
                    if dbg:
                        S.dma("sp", dbg_out["d_x1"][j * 128:(j + 1) * 128, :], x1[:], reads=[x1], writes=[("dx1", j)])
                S.barrier()

        if "p3b" in phases:
            with ExitStack() as p4:
                gffn = sb("gffn", [128, 1024], F32, p4)
                gfin = sb("gfin", [128, 1024], F32, p4)
                S.dma("sp", gffn[:], g_ffn.unsqueeze(0).to_broadcast([128, 1024]), writes=[gffn])
                S.dma("sp", gfin[:], g_final.unsqueeze(0).to_broadcast([128, 1024]), writes=[gfin])
                eiP = Pool(nc, p4, "q_ei", [128, 128], U32, NT_OWN)
                gtP = Pool(nc, p4, "q_gt", [128, 128], F32, NT_OWN)
                hbP = Pool(nc, p4, "q_hb", [128, 1024], BF16, NT_OWN)
                T = pst("p4T", [128, 1024], BF16, p4)
                QP = pst("p4QP", [128, 1024], F32, p4)
                SC = pst("p4SC", [128, 1024], F32, p4)
                Yp = pst("p4Y", [128, 1024], F32, p4)
                with ExitStack() as pr:
                    WP = sb("WP", [128, 8, 2048], BF16, pr)
                    skT = sb("skT", [128, 16, 128], BF16, pr)
                    with ExitStack() as tmp:
                        load_w(tmp, WP, w_pq, 0, 2048, 8)
                        sks = sb("sks", [128, 16, 128], F32, tmp)
                        skb = sb("skb", [128, 16, 128], BF16, tmp)
                        S.dma("sp", sks[:], subk.rearrange("m n d -> n m d"), writes=[sks])
                        CP("dve", skb[:], sks[:], [sks], [skb])
                        for half in range(2):
                            for m in range(8):
                                TR(T[:, m * 128:(m + 1) * 128], skb[:, half * 8 + m, :], [skb], ["T"])
                            CP("dve", skT[:, half * 8:half * 8 + 8, :], T[:, 0:1024].rearrange("p (m t) -> p m t", m=8), ["T"], [skT])
                        S.barrier()
                    x1p = Pool(nc, pr, "q_x1", [128, 1024], F32, 2)
                    hTp = Pool(nc, pr, "q_hT", [128, 8, 128], BF16, 2)
                    ssp = Pool(nc, pr, "q_ss", [128, 1], F32, 4)
                    rsp = Pool(nc, pr, "q_rs", [128, 1], F32, 4)
                    jk = sb("q_jk", [128, 1024], BF16, pr)
                    qpTp = Pool(nc, pr, "qpT", [128, 16, 128], BF16, 2)
                    wkp = Pool(nc, pr, "q_wk", [128, 256], F32, 4)
                    m16p = Pool(nc, pr, "q_m16", [128, 16, 16], F32, 2)
                    i16p = Pool(nc, pr, "q_i16", [128, 16, 16], U32, 2)
                    candp = Pool(nc, pr, "q_cand", [128, 8, 256], F32, 2)
                    c16p = Pool(nc, pr, "q_c16", [128, 8, 16], F32, 2)
                    p16p = Pool(nc, pr, "q_p16", [128, 8, 16], U32, 2)
                    abp = Pool(nc, pr, "q_ab", [128, 2, 128], U32, 2)
                    ohp = Pool(nc, pr, "q_oh", [128, 128, 16], F32, 2)
                    selp = Pool(nc, pr, "q_sel", [128, 2, 128], F32, 2)
                    efp = Pool(nc, pr, "q_ef", [128, 128], F32, 2)
                    zp = Pool(nc, pr, "q_z", [128, 8], F32, 2)
                    def front_gen(j):
                        x1, hb, hT = x1p.get(), hbP.t[j], hTp.get()
                        ss, rs = ssp.get(), rsp.get()
                        S.dma("sp", x1[:], x1s[j * 128:(j + 1) * 128, :], reads=[("x1s", j)], writes=[x1])
                        ACT(jk[:], x1[:], AF.Square, [x1], [jk, ss], accum_out=ss[:])
                        rstd_from_ss(rs, ss, 1024)
                        yield
                        STT(hb[:], x1[:], rs[:, 0:1], gffn[:], ALU.mult, ALU.mult, [x1, rs, gffn], [hb])
                        yield
                        for k in range(8):
                            TR(T[:, k * 128:(k + 1) * 128], hb[:, k * 128:(k + 1) * 128], [hb], ["T"])
                        CP("dve", hT[:], T[:, 0:1024].rearrange("p (k t) -> p k t", k=8), ["T"], [hT])
                        yield
                        qpT = qpTp.get()
                        for half in range(2):
                            for m in range(8):
                                hp = half * 8 + m
                                for k in range(8):
                                    MM(QP[:, m * 128:(m + 1) * 128], WP[:, k, hp * 128:(hp + 1) * 128], hT[:, k, :], k == 0, k == 7,
                                       [WP, hT], ["QP"])
                            CP("act", qpT[:, half * 8:half * 8 + 8, :], QP[:, :].rearrange("p (m t) -> p m t", m=8), ["QP"], [qpT])
                            yield
                        m16, i16 = m16p.get(), i16p.get()
                        for half in range(2):
                            for m in range(8):
                                hp = half * 8 + m
                                MM(SC[:, m * 128:(m + 1) * 128], qpT[:, hp, :], skT[:, hp, :], True, True, [qpT, skT], [("SC", m // 4)])
                            for m0 in range(0, 8, 2):
                                ms = (m0, m0 + 1)
                                hps = [half * 8 + m for m in ms]
                                wks = [wkp.get() for _ in ms]
                                scs = [SC[:, m * 128:(m + 1) * 128] for m in ms]
                                kys = [("SC", m // 4) for m in ms]
                                for i_ in range(2):
                                    S.op("dve", lambda e: e.max(out=m16[:, hps[i_], 0:8], in_=scs[i_]), [kys[i_]], [(id(m16), hps[i_])])
                                for i_ in range(2):
                                    S.op("dve", lambda e: e.max_index(out=i16[:, hps[i_], 0:8], in_max=m16[:, hps[i_], 0:8], in_values=scs[i_]),
                                         [kys[i_], (id(m16), hps[i_])], [(id(i16), hps[i_])])
                                for i_ in range(2):
                                    S.op("dve", lambda e: e.match_replace(out=wks[i_][:, 0:128], in_to_replace=m16[:, hps[i_], 0:8],
                                                                          in_values=scs[i_], imm_value=-1e30),
                                         [kys[i_], (id(m16), hps[i_])], [wks[i_]])
                                for i_ in range(2):
                                    S.op("dve", lambda e: e.max(out=m16[:, hps[i_], 8:16], in_=wks[i_][:, 0:128]), [wks[i_]], [(id(m16), hps[i_], 1)])
                                for i_ in range(2):
                                    S.op("dve", lambda e: e.max_index(out=i16[:, hps[i_], 8:16], in_max=m16[:, hps[i_], 8:16],
                                                                      in_values=wks[i_][:, 0:128]),
                                         [wks[i_], (id(m16), hps[i_], 1)], [(id(i16), hps[i_], 1)])
                            yield
                        cand, c16, p16 = candp.get(), c16p.get(), p16p.get()
                        m4 = m16[:].rearrange("p (h s) k -> p h s k", s=2)
                        m16k = [(id(m16), hp_) for hp_ in range(16)] + [(id(m16), hp_, 1) for hp_ in range(16)]
                        i16k = [(id(i16), hp_) for hp_ in range(16)] + [(id(i16), hp_, 1) for hp_ in range(16)]
                        TT("dve", cand[:].rearrange("p h (a b) -> p h a b", a=16),
                           m4[:, :, 0, :].unsqueeze(3).to_broadcast([128, 8, 16, 16]),
                           m4[:, :, 1, :].unsqueeze(2).to_broadcast([128, 8, 16, 16]), ALU.add, m16k, [cand])
                        yield
                        for h0 in range(0, 8, 2):
                            hs = (h0, h0 + 1)
                            wks = [wkp.get() for _ in hs]
                            cds = [cand[:, h, :] for h in hs]
                            for i_ in range(2):
                                S.op("dve", lambda e: e.max(out=c16[:, hs[i_], 0:8], in_=cds[i_]), [cand], [(id(c16), hs[i_])])
                            for i_ in range(2):
                                S.op("dve", lambda e: e.max_index(out=p16[:, hs[i_], 0:8], in_max=c16[:, hs[i_], 0:8], in_values=cds[i_]),
                                     [cand, (id(c16), hs[i_])], [(id(p16), hs[i_])])
                            for i_ in range(2):
                                S.op("dve", lambda e: e.match_replace(out=wks[i_][:], in_to_replace=c16[:, hs[i_], 0:8], in_values=cds[i_],
                                                                      imm_value=-1e30), [cand, (id(c16), hs[i_])], [wks[i_]])
                            for i_ in range(2):
                                S.op("dve", lambda e: e.max(out=c16[:, hs[i_], 8:16], in_=wks[i_][:]), [wks[i_]], [(id(c16), hs[i_], 1)])
                            for i_ in range(2):
                                S.op("dve", lambda e: e.max_index(out=p16[:, hs[i_], 8:16], in_max=c16[:, hs[i_], 8:16], in_values=wks[i_][:]),
                                     [wks[i_], (id(c16), hs[i_], 1)], [(id(p16), hs[i_], 1)])
                            yield
                        c16k = [(id(c16), h_) for h_ in range(8)] + [(id(c16), h_, 1) for h_ in range(8)]
                        p16k = [(id(p16), h_) for h_ in range(8)] + [(id(p16), h_, 1) for h_ in range(8)]
                        ab, oh, sel, ef, ei = abp.get(), ohp.get(), selp.get(), efp.get(), eiP.t[j]
                        p16f = p16[:].rearrange("p h k -> p (h k)")
                        TS("dve", ab[:, 0, :], p16f, 4, None, ALU.logical_shift_right, None, p16k, [ab])
                        TS("dve", ab[:, 1, :], p16f, 15, None, ALU.bitwise_and, None, p16k, [ab])
                        i4 = i16[:].rearrange("p (h s) k -> p h s k", s=2)
                        for s_ in range(2):
                            TT("dve", oh[:], ab[:, s_, :].unsqueeze(2).to_broadcast([128, 128, 16]),
                               iota16[:].unsqueeze(1).to_broadcast([128, 128, 16]), ALU.is_equal, [ab, iota16], [oh])
                            yield
                            TT("dve", oh[:].rearrange("p (h k) a -> p h k a", h=8), oh[:].rearrange("p (h k) a -> p h k a", h=8),
                               i4[:, :, s_, :].unsqueeze(2).to_broadcast([128, 8, 16, 16]), ALU.mult, [oh] + i16k, [oh])
                            yield
                            S.op("dve", lambda e: e.tensor_reduce(out=sel[:, s_, :], in_=oh[:], axis=AX.X, op=ALU.add),
                                 [oh], [sel])
                            yield
                        STT(ef[:], sel[:, 0, :], 128.0, sel[:, 1, :], ALU.mult, ALU.add, [sel], [ef])
                        CP("dve", ei[:], ef[:], [ef], [ei])
                        gt, z = gtP.t[j], zp.get()
                        TT("dve", gt[:].rearrange("p (h k) -> p h k", h=8), c16[:], c16[:, :, 0:1].to_broadcast([128, 8, 16]),
                           ALU.subtract, c16k, [gt])
                        ACT(gt[:], gt[:], AF.Exp, [gt], [gt])
                        S.op("dve", lambda e: e.tensor_reduce(out=z[:], in_=gt[:].rearrange("p (h k) -> p h k", h=8), axis=AX.X,
                                                              op=ALU.add), [gt], [z])
                        S.op("dve", lambda e: e.reciprocal(out=z[:], in_=z[:]), [z], [z])
                        TT("dve", gt[:].rearrange("p (h k) -> p h k", h=8), gt[:].rearrange("p (h k) -> p h k", h=8),
                           z[:].unsqueeze(2).to_broadcast([128, 8, 16]), ALU.mult, [gt, z], [gt])
                        return None


                    gens = [front_gen(j) for j in range(n_own)]
                    active, nx = [], 0
                    while nx < n_own or active:
                        while len(active) < 2 and nx < n_own:
                            active.append(gens[nx])
                            nx += 1
                        for g_ in list(active):
                            try:
                                next(g_)
                            except StopIteration:
                                active.remove(g_)
                    S.barrier()
                S.wait_bg()
                with ExitStack() as pe:
                    x1p = Pool(nc, pe, "e_x1", [128, 1024], F32, 2)
                    ssp = Pool(nc, pe, "e_ss", [128, 1], F32, 2)
                    rsp = Pool(nc, pe, "e_rs", [128, 1], F32, 2)
                    jk = sb("e_jk", [128, 1024], BF16, pe)
                    ap_ = Pool(nc, pe, "q_a", [128, 128], F32, 2)
                    cp_ = Pool(nc, pe, "q_c", [128, 128], F32, 2)
                    gbp = Pool(nc, pe, "q_gb", [128, 2048], BF16, 16)
                    Dp4 = Pool(nc, pe, "q_D", [128, 8, 128], BF16, 3)
                    jk2 = sb("q_jk2", [128, 1024], BF16, pe)
                    yp = Pool(nc, pe, "q_y", [128, 1024], F32, 2)
                    for j in range(n_own):
                        x1, hb, ei, gt = x1p.get(), hbP.t[j], eiP.t[j], gtP.t[j]
                        S.dma("sp", x1[:], x1s[j * 128:(j + 1) * 128, :], reads=[("x1s", j)], writes=[x1])
                        a_, c_ = ap_.get(), cp_.get()
                        for grp in range(16):
                            gbs = []
                            for kk in range(8):
                                e_ = grp * 8 + kk
                                gb = gbp.get()
                                gbs.append(gb)
                                S.dma("pool", gb[:], uvb[:, :], reads=[ei], writes=[gb],
                                      indirect=bass.IndirectOffsetOnAxis(ap=ei[:, e_:e_ + 1], axis=0))
                                S.op("dve", lambda e: e.scalar_tensor_tensor(out=jk2[:], in0=gb[:, 0:1024], scalar=1.0, in1=hb[:],
                                                                             op0=ALU.mult, op1=ALU.mult,
                                                                             accum_out=a_[:, e_:e_ + 1]),
                                     [gb, hb], [jk2, ("a", j, grp)])
                            g0 = grp * 8
                            ACT(c_[:, g0:g0 + 8], a_[:, g0:g0 + 8], AF.Gelu, [("a", j, grp)], [("c", j, grp)])
                            TT("dve", c_[:, g0:g0 + 8], c_[:, g0:g0 + 8], gt[:, g0:g0 + 8], ALU.mult, [("c", j, grp), gt], [("c", j, grp)])
                            Dk = Dp4.get()
                            for kk in range(8):
                                ACT(Dk[:, kk, :], identb[:], AF.Copy, [identb, ("c", j, grp)], [Dk], scale=c_[:, g0 + kk:g0 + kk + 1])
                            for kk in range(8):
                                e_ = grp * 8 + kk
                                for n_ in range(2):
                                    MM(Yp[:, n_ * 512:(n_ + 1) * 512], Dk[:, kk, :], gbs[kk][:, 1024 + n_ * 512:1024 + (n_ + 1) * 512],
                                       e_ == 0, e_ == 127, [Dk, gbs[kk]], [("Y", n_)])
                        y = yp.get()
                        TT("dve", y[:], x1[:], Yp[:, :], ALU.add, [x1, ("Y", 0), ("Y", 1)], [y])
                        ss2, rs2 = ssp.get(), rsp.get()
                        ACT(jk[:], y[:], AF.Square, [y], [jk, ss2], accum_out=ss2[:])
                        rstd_from_ss(rs2, ss2, 1024)
                        STT(y[:], y[:], rs2[:, 0:1], gfin[:], ALU.mult, ALU.mult, [y, rs2, gfin], [y])
                        S.dma("sp", out[j * 128:(j + 1) * 128, :], y[:], reads=[y], writes=[("out", j)])
                    S.barrier()
        S.barrier()
        print("program: insts=%d waits=%d sems=%d" % (S.n_inst, S.n_wait, S.nsem))
    return nc


def _consts():
    t = np.arange(128)
    same = (t[:, None] // 64) == (t[None, :] // 64)
    rem = (same & (t[:, None] > t[None, :])).astype(np.float32) / 16.0
    cind = np.stack([(t < 64), (t >= 64)], axis=1).astype(np.float32)
    iota = np.tile(np.arange(16, dtype=np.float32)[None, :], (128, 1))
    invf = (10000.0 ** (-np.arange(16, dtype=np.float32) / np.float32(16))).astype(np.float32)
    invf = np.tile(invf[None, :], (128, 1)).astype(np.float32)
    return {"c_ident": np.eye(128, dtype=np.float32), "c_rem": rem, "c_cind": cind, "c_cind16": cind / 16.0,
            "c_iota": iota, "c_invf": invf}


def _masks(r):
    m = np.zeros((128, 4, 128), np.float32)
    k = np.arange(128)
    for t in range(4):
        if t < r:
            m[:, t, :] = 1.0
        elif t == r:
            m[:, t, :] = ((k[:, None] // 64) <= (k[None, :] // 64)).astype(np.float32)
    return m


def make_in_maps(inputs):
    x = np.asarray(inputs["x"], np.float32)
    pos = np.asarray(inputs["positions"], np.int32)
    cst = _consts()
    shared = {
        "g_mix": inputs["g_mix"].reshape(1024), "w_in": inputs["w_in"].reshape(1024, 4272),
        "g_q_lat": inputs["g_q_lat"].reshape(384), "w_qb": inputs["w_qb"].reshape(384, 768),
        "g_kv_lat": inputs["g_kv_lat"].reshape(256), "w_kvb": inputs["w_kvb"].reshape(256, 1024),
        "w_a2": inputs["w_a2"].reshape(16, 256), "b_a2": inputs["b_a2"].reshape(256),
        "g_gla": inputs["g_gla"].reshape(512), "w_branch_a": inputs["w_branch_a"].reshape(512, 1024),
        "w_branch_b": inputs["w_branch_b"].reshape(512, 1024), "w_out": inputs["w_out"].reshape(1024, 1024),
        "g_ffn": inputs["g_ffn"].reshape(1024), "w_peer_q": inputs["w_peer_q"].reshape(1024, 2048),
        "peer_sub_keys": inputs["peer_sub_keys"].reshape(16, 128, 128),
        "peer_u": inputs["peer_u"].reshape(16384, 1024), "peer_v": inputs["peer_v"].reshape(16384, 1024),
        "g_final": inputs["g_final"].reshape(1024),
    }
    shared = {k: np.ascontiguousarray(np.asarray(v, np.float32)) for k, v in shared.items()}
    shared.update(cst)
    maps = []
    for c in range(8):
        b, r = c // 4, c % 4
        xt = x[b].reshape(64, 128, 1024)
        own = np.arange(16) * 4 + r
        pt = pos[b].reshape(64, 128)
        p = np.arange(128)
        cidx = np.zeros((128, 32), np.uint32)
        for j in range(16):
            for cc in range(2):
                cidx[:, 2 * j + cc] = (8 * j + 2 * r + cc) * 128 + p
        m = dict(shared)
        m["xb"] = np.ascontiguousarray(x[b])
        m["xo"] = np.ascontiguousarray(xt[own].reshape(2048, 1024))
        m["posb"] = np.ascontiguousarray(pt.T)
        m["poso"] = np.ascontiguousarray(pt[own].T)
        m["cidx"] = cidx
        m["masks"] = _masks(r)
        maps.append(m)
    return maps


def assemble(results):
    outp = np.zeros((2, 64, 128, 1024), np.float32)
    for c in range(8):
        b, r = c // 4, c % 4
        own = np.arange(16) * 4 + r
        outp[b, own] = np.asarray(results[c]["out"], np.float32).reshape(16, 128, 1024)
    return outp.reshape(2, 8192, 1024)


def kernel(**inputs):
    nc = build_program()
    maps = make_in_maps(inputs)
    res = run_bass_kernel_spmd(nc, maps, core_ids=list(range(8)))
    return assemble(res.results)
```

```python
import math
from contextlib import ExitStack

import numpy as np
import concourse.bass as bass
import concourse.mybir as mybir
from concourse.bass_utils import run_bass_kernel_spmd

F32 = mybir.dt.float32
BF16 = mybir.dt.bfloat16
I32 = mybir.dt.int32
U32 = mybir.dt.uint32
AF = mybir.ActivationFunctionType
ALU = mybir.AluOpType
AX = mybir.AxisListType

SEM_LIMIT = 30000
EPS = 1e-6
NT_ALL = 64
NT_OWN = 16
PI = math.pi


class Sched:
    def __init__(self, nc, es, n_dma_sems=16):
        self.nc = nc
        self.es = es
        self.eng = {"pe": nc.tensor, "dve": nc.vector, "act": nc.scalar,
                    "pool": nc.gpsimd, "sp": nc.sync}
        self.nsem = 0
        self.sem = {}
        self.cnt = {}
        self.all_sems = []
        for e in ("pe", "dve", "act", "pool"):
            self._new_eng_sem(e)
        self.seen = {e: {} for e in self.eng}
        self.dsem = {"sp": [], "pool": []}
        self.dcnt = {"sp": [], "pool": []}
        self.drr = {"sp": 0, "pool": 0}
        for q, n in (("sp", n_dma_sems), ("pool", 8)):
            for i in range(n):
                self.dsem[q].append(self._alloc_sem())
                self.dcnt[q].append(0)
        self.bg = []
        self.res = {}
        self.excl = set()
        self.n_inst = 0
        self.n_wait = 0

    def _alloc_sem(self):
        s = self.es.enter_context(self.nc.semaphore("s%d" % self.nsem))
        self.nsem += 1
        return s

    def _new_eng_sem(self, e):
        if e in self.sem:
            self.all_sems.append((self.sem[e], self.cnt[e]))
        self.sem[e] = self._alloc_sem()
        self.cnt[e] = 0

    def _wait(self, e, tok):
        if tok is None:
            return
        s, v = tok
        if v <= 0:
            return
        k = id(s)
        d = self.seen[e]
        if d.get(k, 0) >= v:
            return
        self.eng[e].wait_ge(s, v)
        self.n_wait += 1
        d[k] = v

    @staticmethod
    def _k(x):
        return x if isinstance(x, (str, tuple, int)) else id(x)

    def _deps(self, e, reads, writes):
        toks = []
        own = self.sem.get(e)
        for r in reads:
            st = self.res.get(r)
            if st is None:
                continue
            if st["w"] is not None:
                toks.append(st["w"])
            if r in self.excl:
                for t in st["r"].values():
                    if not (t[0] is own):
                        toks.append(t)
        for w in writes:
            st = self.res.get(w)
            if st is None:
                continue
            if st["w"] is not None and not (e == "pe" and st["w"][0] is own):
                toks.append(st["w"])
            for t in st["r"].values():
                toks.append(t)
        for t in toks:
            self._wait(e, t)

    def _update(self, tok, reads, writes):
        for r in reads:
            st = self.res.setdefault(r, {"w": None, "r": {}})
            st["r"][id(tok[0])] = tok
        for w in writes:
            self.res[w] = {"w": tok, "r": {}}

    def op(self, e, fn, reads=(), writes=()):
        reads = [self._k(x) for x in reads]
        writes = [self._k(x) for x in writes]
        if self.cnt[e] >= SEM_LIMIT:
            self._new_eng_sem(e)
        self._deps(e, reads, writes)
        ins = fn(self.eng[e])
        self.cnt[e] += 1
        ins.then_inc(self.sem[e], 1)
        tok = (self.sem[e], self.cnt[e])
        self._update(tok, reads, writes)
        self.n_inst += 1
        return tok

    def dma(self, q, out, in_, reads=(), writes=(), indirect=None, slow=False):
        reads = [self._k(x) for x in reads]
        writes = [self._k(x) for x in writes]
        dsem, dcnt = self.dsem[q], self.dcnt[q]
        i = self.drr[q]
        self.drr[q] = (i + 1) % len(dsem)
        if dcnt[i] >= SEM_LIMIT:
            self._wait(q, (dsem[i], dcnt[i]))
            self.all_sems.append((dsem[i], dcnt[i]))
            dsem[i] = self._alloc_sem()
            dcnt[i] = 0
        s = dsem[i]
        self._wait(q, (s, dcnt[i]))
        self._deps(q, reads, writes)
        if indirect is not None:
            ins = self.eng[q].indirect_dma_start(out=out, out_offset=None, in_=in_,
                                                 in_offset=indirect)
        else:
            ins = self.eng[q].dma_start(out=out, in_=in_, allow_slow_non_contiguous=True) if slow else self.eng[q].dma_start(out=out, in_=in_)
        ins.then_inc(s, 16)
        dcnt[i] += 16
        tok = (s, dcnt[i])
        self._update(tok, reads, writes)
        self.n_inst += 1
        return tok

    def dma_bg(self, out, in_):
        s = self._alloc_sem()
        self.eng["pool"].dma_start(out=out, in_=in_).then_inc(s, 16)
        self.bg.append((s, 16))
        self.n_inst += 1

    def wait_bg(self):
        for e in self.eng:
            for t in self.bg:
                self._wait(e, t)

    def barrier(self):
        toks = [(self.sem[e], self.cnt[e]) for e in self.sem]
        for q in self.dsem:
            toks += [(s, c) for s, c in zip(self.dsem[q], self.dcnt[q])]
        for e in self.eng:
            for t in toks:
                self._wait(e, t)
        self.res = {}


class Pool:
    def __init__(self, nc, es, name, shape, dtype, bufs, psum=False):
        self.t = []
        for i in range(bufs):
            if psum:
                t = es.enter_context(nc.psum_tensor("%s%d" % (name, i), shape, dtype))
            else:
                t = es.enter_context(nc.sbuf_tensor("%s%d" % (name, i), shape, dtype))
            self.t.append(t)
        self.i = 0

    def get(self):
        t = self.t[self.i]
        self.i = (self.i + 1) % len(self.t)
        return t


C_QL = (0, 384)
C_KV = (384, 672)
C_GQ = (672, 928)
C_GKVL = (928, 1712)
C_GO_BR = (1712, 4272)


def build_program(phases=("p0", "p1", "p2", "p3a", "p3b"), n_all=NT_ALL, n_own=NT_OWN, dbg=False, stop=99):
    nc = bass.Bass("TRN2", target_bir_lowering=False)

    def din(name, shape, dt=F32):
        return nc.dram_tensor(name, list(shape), dt, kind="ExternalInput").ap()

    xb = din("xb", [8192, 1024])
    xo = din("xo", [2048, 1024])
    posb = din("posb", [128, 64], I32)
    poso = din("poso", [128, 16], I32)
    cidx = din("cidx", [128, 32], U32)
    masks = din("masks", [128, 4, 128])
    c_ident = din("c_ident", [128, 128])
    c_rem = din("c_rem", [128, 128])
    c_cind = din("c_cind", [128, 2])
    c_cind16 = din("c_cind16", [128, 2])
    c_iota = din("c_iota", [128, 16])
    c_invf = din("c_invf", [128, 16])
    g_mix = din("g_mix", [1024])
    w_in = din("w_in", [1024, 4272])
    g_q_lat = din("g_q_lat", [384])
    w_qb = din("w_qb", [384, 768])
    g_kv_lat = din("g_kv_lat", [256])
    w_kvb = din("w_kvb", [256, 1024])
    w_a2 = din("w_a2", [16, 256])
    b_a2 = din("b_a2", [256])
    g_gla = din("g_gla", [512])
    w_ba = din("w_branch_a", [512, 1024])
    w_bb = din("w_branch_b", [512, 1024])
    w_out = din("w_out", [1024, 1024])
    g_ffn = din("g_ffn", [1024])
    w_pq = din("w_peer_q", [1024, 2048])
    subk = din("peer_sub_keys", [16, 128, 128])
    peer_u = din("peer_u", [16384, 1024])
    peer_v = din("peer_v", [16384, 1024])
    g_final = din("g_final", [1024])
    out = nc.dram_tensor("out", [2048, 1024], F32, kind="ExternalOutput").ap()
    states = nc.dram_tensor("states", [128 * 128, 256], F32, kind="Internal").ap()
    x1s = nc.dram_tensor("x1s", [2048, 1024], F32, kind="Internal").ap()
    yaTs = nc.dram_tensor("yaTs", [NT_OWN * 128, 512], BF16, kind="Internal").ap()
    uvb = nc.dram_tensor("peer_uv_bf", [16384, 2048], BF16, kind="Internal").ap()
    dbg_out = {}
    if dbg:
        dbg_out["d_states"] = nc.dram_tensor("d_states", [128 * 128, 256], F32, kind="ExternalOutput").ap()
        dbg_out["d_ya"] = nc.dram_tensor("d_ya", [2048, 512], F32, kind="ExternalOutput").ap()
        dbg_out["d_x1"] = nc.dram_tensor("d_x1", [2048, 1024], F32, kind="ExternalOutput").ap()
        dbg_out["d_yb"] = nc.dram_tensor("d_yb", [2048, 512], F32, kind="ExternalOutput").ap()

    with ExitStack() as es:
        S = Sched(nc, es)

        def sb(name, shape, dt, st=es):
            return st.enter_context(nc.sbuf_tensor(name, shape, dt))

        def pst(name, shape, dt, st):
            return st.enter_context(nc.psum_tensor(name, shape, dt))

        S.excl.update(["T", "B", "C", "G", ("D", 0), ("D", 1), "A", "T2", "QL", ("Q", 0), ("Q", 1), "G1", "GO0", "GO1",
                       ("BR", 0), ("BR", 1), ("BR", 2), ("BR", 3), "QP", ("SC", 0), ("SC", 1), ("Y", 0), ("Y", 1)])

        def ACT(o, i, func, r, w, **kw):
            S.op("act", lambda e: e.activation(out=o, in_=i, func=func, **kw), r, w)

        def MM(o, lhsT, rhs, start, stop, r, w):
            S.op("pe", lambda e: e.matmul(o, lhsT=lhsT, rhs=rhs, start=start, stop=stop), r, w)

        def TT(eng, o, a, b, op, r, w):
            S.op(eng, lambda e: e.tensor_tensor(out=o, in0=a, in1=b, op=op), r, w)

        def TS(eng, o, a, s1, s2, op0, op1, r, w):
            if op1 is None:
                S.op(eng, lambda e: e.tensor_scalar(out=o, in0=a, scalar1=s1, scalar2=None, op0=op0), r, w)
            else:
                S.op(eng, lambda e: e.tensor_scalar(out=o, in0=a, scalar1=s1, scalar2=s2, op0=op0, op1=op1), r, w)

        def STT(o, a, sc, b, op0, op1, r, w):
            S.op("dve", lambda e: e.scalar_tensor_tensor(out=o, in0=a, scalar=sc, in1=b, op0=op0, op1=op1), r, w)

        def CP(eng, o, i, r, w):
            if eng == "act":
                S.op("act", lambda e: e.copy(out=o, in_=i), r, w)
            else:
                S.op(eng, lambda e: e.tensor_copy(out=o, in_=i), r, w)

        identf = sb("identf", [128, 128], F32)
        identb = sb("identb", [128, 128], BF16)
        remf = sb("remf", [128, 128], F32)
        cindf = sb("cindf", [128, 2], F32)
        cind16 = sb("cind16", [128, 2], F32)
        iota16 = sb("iota16", [128, 16], F32)
        invf = sb("invf", [128, 16], F32)
        gmix = sb("gmix", [128, 8], F32)
        S.dma("sp", identf[:], c_ident[:, :], writes=[identf])
        S.dma("sp", remf[:], c_rem[:, :], writes=[remf])
        S.dma("sp", cindf[:], c_cind[:, :], writes=[cindf])
        S.dma("sp", cind16[:], c_cind16[:, :], writes=[cind16])
        S.dma("sp", iota16[:], c_iota[:, :], writes=[iota16])
        S.dma("sp", invf[:], c_invf[:, :], writes=[invf])
        S.dma("sp", gmix[:], g_mix.rearrange("(k p) -> p k", p=128), writes=[gmix], slow=True)
        CP("dve", identb[:], identf[:], [identf], [identb])

        if "p3b" in phases:
            for c in range(16):
                S.dma_bg(uvb[c * 1024:(c + 1) * 1024, 0:1024], peer_u[c * 1024:(c + 1) * 1024, :])
            for c in range(16):
                S.dma_bg(uvb[c * 1024:(c + 1) * 1024, 1024:2048], peer_v[c * 1024:(c + 1) * 1024, :])

        def TR(o, i, r, w):
            S.op("pe", lambda e: e.transpose(out=o, in_=i, identity=identb[:]), list(r) + [identb], w)

        cvt_rr = [0, 0]

        def load_w(st, dst, wd, c0, c1, nk, gain=None, dst_off=0):
            cvt_rr[1] += 1
            stage = Pool(nc, st, "wstage%d_" % cvt_rr[1], [128, 1024], F32, 3)
            n = c1 - c0
            for k in range(nk):
                for s0 in range(0, n, 1024):
                    m = min(1024, n - s0)
                    sg = stage.get()
                    S.dma("sp", sg[:, 0:m], wd[k * 128:(k + 1) * 128, c0 + s0:c0 + s0 + m], writes=[sg])
                    o = dst[:, k, dst_off + s0:dst_off + s0 + m]
                    eng = ("dve", "act")[cvt_rr[0] % 2]
                    cvt_rr[0] += 1
                    if gain is not None:
                        if eng == "dve":
                            TS(eng, o, sg[:, 0:m], gain[:, k:k + 1], None, ALU.mult, None, [sg, gain], [dst])
                        else:
                            ACT(o, sg[:, 0:m], AF.Copy, [sg, gain], [dst], scale=gain[:, k:k + 1])
                    else:
                        CP(eng, o, sg[:, 0:m], [sg], [dst])

        def rstd_from_ss(rs, ss, n, cols=1):
            ACT(rs[:], ss[:], AF.Ln, [ss, eps_t], [rs], scale=1.0 / n, bias=eps_t[:, 0:1])
            ACT(rs[:], rs[:], AF.Exp, [rs], [rs], scale=-0.5)

        eps_t = sb("eps_t", [128, 1], F32)
        one_t = sb("one_t", [128, 1], F32)
        S.op("dve", lambda e: e.memset(eps_t[:], EPS), [], [eps_t])
        S.op("dve", lambda e: e.memset(one_t[:], 1.0), [], [one_t])

        class Front:
            def __init__(self, st, Tps, tkey, nb=2):
                cvt_rr[1] += 1
                u = "f%d_" % cvt_rr[1]
                self.xt = Pool(nc, st, u + "xt", [128, 1024], F32, nb)
                self.hb = Pool(nc, st, u + "hb", [128, 1024], BF16, nb)
                self.hT = Pool(nc, st, u + "hT", [128, 8, 128], BF16, nb)
                self.ss = Pool(nc, st, u + "ss", [128, 1], F32, 2)
                self.rs = Pool(nc, st, u + "rs", [128, 1], F32, 2)
                self.Tps = Tps
                self.tkey = tkey

            def __call__(self, src):
                xt, hb, hT, ss, rs = self.xt.get(), self.hb.get(), self.hT.get(), self.ss.get(), self.rs.get()
                S.dma("sp", xt[:], src, writes=[xt])
                ACT(hb[:], xt[:], AF.Square, [xt], [hb, ss], accum_out=ss[:])
                rstd_from_ss(rs, ss, 1024)
                ACT(hb[:], xt[:], AF.Copy, [xt, rs], [hb], scale=rs[:])
                T = self.Tps
                for k in range(8):
                    TR(T[:, k * 128:(k + 1) * 128], hb[:, k * 128:(k + 1) * 128], [hb], [self.tkey])
                CP("dve", hT[:], T[:, 0:1024].rearrange("p (k t) -> p k t", k=8), [self.tkey], [hT])
                return xt, hT

        def trig_tables(cos_t, sin_t, tmp, pos_d, n, scale, name):
            pi_ = sb(name + "_pi", [128, n], I32, tmp)
            pf = sb(name + "_pf", [128, n], F32, tmp)
            ang = sb(name + "_ang", [128, n, 16], F32, tmp)
            ki = sb(name + "_ki", [128, n, 16], I32, tmp)
            kf = sb(name + "_kf", [128, n, 16], F32, tmp)
            r0 = sb(name + "_r0", [128, n, 16], F32, tmp)
            y = sb(name + "_y", [128, n, 16], F32, tmp)
            m = sb(name + "_m", [128, n, 16], F32, tmp)
            S.dma("sp", pi_[:], pos_d, writes=[pi_])
            CP("dve", pf[:], pi_[:], [pi_], [pf])
            TT("dve", ang[:], pf[:].unsqueeze(2).to_broadcast([128, n, 16]),
               invf[:].unsqueeze(1).to_broadcast([128, n, 16]), ALU.mult, [pf, invf], [ang])
            TS("dve", ki[:], ang[:], 1.0 / (2 * PI), None, ALU.mult, None, [ang], [ki])
            CP("dve", kf[:], ki[:], [ki], [kf])
            c1 = 6.28125
            c2 = float(np.float32(2 * PI - c1))
            c3 = float(2 * PI - c1 - c2)
            STT(r0[:], kf[:], -c1, ang[:], ALU.mult, ALU.add, [kf, ang], [r0])
            STT(r0[:], kf[:], -c2, r0[:], ALU.mult, ALU.add, [kf, r0], [r0])
            STT(r0[:], kf[:], -c3, r0[:], ALU.mult, ALU.add, [kf, r0], [r0])
            for shift, dst in ((0.0, sin_t), (PI / 2, cos_t)):
                TS("dve", y[:], r0[:], shift, None, ALU.add, None, [r0], [y])
                for _ in range(2):
                    TS("dve", m[:], y[:], PI, -2 * PI, ALU.is_gt, ALU.mult, [y], [m])
                    TT("dve", y[:], y[:], m[:], ALU.add, [y, m], [y])
                    TS("dve", m[:], y[:], -PI, 2 * PI, ALU.is_lt, ALU.mult, [y], [m])
                    TT("dve", y[:], y[:], m[:], ALU.add, [y, m], [y])
                TS("dve", y[:], y[:], PI, -PI, ALU.min, ALU.max, [y], [y])
                ACT(dst[:], y[:], AF.Sin, [y], [dst])
                if scale != 1.0:
                    TS("dve", dst[:], dst[:], scale, None, ALU.mult, None, [dst], [dst])

        if "p0" in phases:
            with ExitStack() as p0:
                W1b = sb("W1b", [128, 8, 784], BF16, p0)
                wa2b = sb("wa2b", [32, 256], BF16, p0)
                if True:
                    tmp = p0
                    load_w(tmp, W1b, w_in, C_GKVL[0], C_GKVL[1], 8, gmix)
                    wa2s = sb("wa2s", [32, 256], F32, tmp)
                    S.dma("sp", wa2s[0:16, :], w_a2[:, :], writes=[wa2s])
                    S.dma("sp", wa2s[16:17, :], b_a2.unsqueeze(0), writes=[wa2s])
                    CP("dve", wa2b[0:17, :], wa2s[0:17, :], [wa2s], [wa2b])
                    S.barrier()
                T = pst("p0T", [128, 1024], BF16, p0)
                Bp = pst("p0B", [128, 512], F32, p0)
                Cp = pst("p0C", [128, 512], F32, p0)
                Gp = pst("p0G", [128, 512], F32, p0)
                Dp = pst("p0D", [128, 1024], F32, p0)
                front = Front(p0, T, "T")
                glrT = Pool(nc, p0, "glrT", [32, 128], BF16, 2)
                for t in glrT.t:
                    S.op("pool", lambda e: e.memset(t[:], 1.0), [], [t])
                e1p = Pool(nc, p0, "e1", [128, 256], F32, 2)
                lap = Pool(nc, p0, "la", [128, 256], F32, 2)
                dkp = Pool(nc, p0, "dk", [128, 256], F32, 2)
                decp = Pool(nc, p0, "dec", [128, 4], F32, 2)
                kdp = Pool(nc, p0, "kd", [128, 2, 256], BF16, 2)
                gvp = Pool(nc, p0, "gv", [128, 512], BF16, 2)
                stp = Pool(nc, p0, "st", [128, 2, 128], F32, 4)
                old = stp.get()
                S.op("dve", lambda e: e.memset(old[:], 0.0), [], [old])
                st_old = [old]
                pend_b = [None]
                nxt = front(xb[0:128, :]) if n_all > 0 else None
                for i in range(n_all):
                    if stop == 1:
                        break
                    xt, hT = nxt
                    if stop == 2:
                        break
                    for k in range(8):
                        MM(Bp[:, :], hT[:, k, :], W1b[:, k, 0:512], k == 0, k == 7, [hT, W1b], ["B"])
                    for k in range(8):
                        MM(Cp[:, 0:256], hT[:, k, :], W1b[:, k, 512:768], k == 0, k == 7, [hT, W1b], ["C"])
                    for k in range(8):
                        MM(Gp[0:16, 0:128], W1b[:, k, 768:784], hT[:, k, :], k == 0, k == 7, [hT, W1b], ["G"])
                    if i + 1 < n_all:
                        nxt = front(xb[(i + 1) * 128:(i + 2) * 128, :])
                    g1 = glrT.get()
                    CP("act", g1[0:16, :], Gp[0:16, 0:128], ["G"], [g1])
                    MM(Cp[:, 256:512], g1[0:17, :], wa2b[0:17, :], True, True, [g1, wa2b], ["C"])
                    if pend_b[0] is not None:
                        pend_b[0]()
                        pend_b[0] = None
                    if stop == 3:
                        break
                    e1, la, dk, dec, kd, gv = e1p.get(), lap.get(), dkp.get(), decp.get(), kdp.get(), gvp.get()
                    ACT(e1[:], Cp[:, 256:512], AF.Exp, ["C"], [e1], scale=-1.0)
                    ACT(la[:], e1[:], AF.Ln, [e1], [la], bias=one_t[:, 0:1])
                    if stop == 31:
                        break
                    MM(Gp[:, 128:384], remf[:], la[:], True, True, [remf, la], ["G"])
                    for p in range(2):
                        MM(Gp[:, 384 + 2 * p:386 + 2 * p], la[:, p * 128:(p + 1) * 128], cind16[:], True, True,
                           [la, cind16], ["G"])
                    if stop == 32:
                        break
                    ACT(dk[:], Gp[:, 128:384], AF.Exp, ["G"], [dk], scale=-1.0)
                    if stop == 331:
                        break
                    ACT(dec[:], Gp[:, 384:388], AF.Exp, ["G"], [dec], scale=-1.0)
                    if stop == 332:
                        break
                    if stop == 33:
                        break
                    for c in range(2):
                        STT(kd[:, c, :], dk[:], cindf[:, c:c + 1], Bp[:, 0:256], ALU.mult, ALU.mult,
                            [dk, cindf, "B"], [kd])
                    if stop == 34:
                        break
                    CP("act", gv[:, 0:256], Bp[:, 256:512], ["B"], [gv])
                    CP("act", gv[:, 256:512], Cp[:, 0:256], ["C"], [gv])
                    def part_b(i=i, kd=kd, gv=gv, dec=dec):
                        for c in range(2):
                            for p in range(2):
                                MM(Dp[:, (c * 2 + p) * 256:(c * 2 + p + 1) * 256], kd[:, c, p * 128:(p + 1) * 128],
                                   gv[:, p * 256:(p + 1) * 256], True, True, [kd, gv], [("D", c)])
                        for c in range(2):
                            new = stp.get()
                            old = st_old[0]
                            for p in range(2):
                                for q in range(2):
                                    rows = slice(q * 64, (q + 1) * 64)
                                    c0 = (c * 2 + p) * 256 + q * 128
                                    STT(new[rows, p, :], old[rows, p, :], dec[rows, p * 2 + c:p * 2 + c + 1],
                                        Dp[rows, c0:c0 + 128], ALU.mult, ALU.add, [old, dec, ("D", c)], [new])
                            ch = 2 * i + c
                            S.dma("sp", states[ch * 128:(ch + 1) * 128, :].rearrange("p (a b) -> p a b", a=2),
                                  new[:], reads=[new], writes=[("states", ch)])
                            if dbg:
                                S.dma("sp", dbg_out["d_states"][ch * 128:(ch + 1) * 128, :].rearrange("p (a b) -> p a b", a=2),
                                      new[:], reads=[new], writes=[("dstates", ch)])
                            st_old[0] = new
                    pend_b[0] = part_b
                if pend_b[0] is not None:
                    pend_b[0]()
                S.barrier()

        if "p1" in phases or "p2" in phases:
            with ExitStack() as kv:
                cosq = sb("tq_cos", [128, NT_OWN, 16], F32, kv)
                sinq = sb("tq_sin", [128, NT_OWN, 16], F32, kv)
                with ExitStack() as tmp:
                    trig_tables(cosq, sinq, tmp, poso[:, :], NT_OWN, 96 ** -0.5, "tq")
                    S.barrier()
                KnT = sb("KnT", [128, 4, 8192], BF16, kv)
                KpeT = sb("KpeT", [128, 8192], BF16, kv)
                V = sb("V", [128, NT_ALL, 8, 66], BF16, kv)
                with ExitStack() as p1:
                    W1a = sb("W1a", [128, 8, 288], BF16, p1)
                    Wkn = sb("Wkn", [128, 2, 512], BF16, p1)
                    Wkv = sb("Wkv", [128, 2, 512], BF16, p1)
                    gkv = sb("gkv", [128, 2], F32, p1)
                    cosb = sb("tb_cos", [128, NT_ALL, 16], F32, p1)
                    sinb = sb("tb_sin", [128, NT_ALL, 16], F32, p1)
                    with ExitStack() as tmp:
                        trig_tables(cosb, sinb, tmp, posb[:, :], NT_ALL, 1.0, "tb")
                        S.barrier()
                    S.dma("sp", gkv[:], g_kv_lat.rearrange("(k p) -> p k", p=128), writes=[gkv], slow=True)
                    if stop > 10:
                        S.op("pool", lambda e: e.memset(V[:], 1.0), [], ["Vinit"])
                    with ExitStack() as tmp:
                        if stop > 11:
                            load_w(tmp, W1a, w_in, C_KV[0], C_KV[1], 8, gmix)
                        kvs = sb("kvs", [128, 2, 8, 2, 64], F32, tmp)
                        for k in range(2 if stop > 11 else 0):
                            S.dma("sp", kvs[:, k], w_kvb[k * 128:(k + 1) * 128, :].rearrange("p (h t d) -> p h t d", h=8, t=2),
                                  writes=[kvs])
                        for k in range(2 if stop > 11 else 0):
                            TS("dve", Wkn[:, k, :].rearrange("p (h d) -> p h d", h=8), kvs[:, k, :, 0, :], gkv[:, k:k + 1], None,
                               ALU.mult, None, [kvs, gkv], [Wkn])
                            TS("pool", Wkv[:, k, :].rearrange("p (h d) -> p h d", h=8), kvs[:, k, :, 1, :], gkv[:, k:k + 1], None,
                               ALU.mult, None, [kvs, gkv], [Wkv])
                        S.barrier()
                    T = pst("p1T", [128, 1024], BF16, p1)
                    Ap = pst("p1A", [128, 512], F32, p1)
                    T2 = pst("p1T2", [128, 1024], BF16, p1)
                    KVp = Pool(nc, p1, "p1KV", [128, 512], F32, 2, psum=True)
                    S.excl.update(id(t) for t in KVp.t)
                    front = Front(p1, T, "T")
                    jk = sb("p1jk", [128, 256], BF16, p1)
                    ssp = Pool(nc, p1, "p1ss", [128, 1], F32, 2)
                    rsp = Pool(nc, p1, "p1rs", [128, 1], F32, 2)
                    kvnp = Pool(nc, p1, "kvn", [128, 256], BF16, 2)
                    kvnTp = Pool(nc, p1, "kvnT", [128, 2, 128], BF16, 2)
                    tp = Pool(nc, p1, "p1t", [128, 4, 16], F32, 2)
                    kpp = Pool(nc, p1, "kp", [128, 32], F32, 2)
                    kp4p = Pool(nc, p1, "kp4", [128, 4, 32], BF16, 2)
                    n_p1 = n_all if ("p1" in phases and stop > 12) else 0
                    nxt = front(xb[0:128, :]) if n_p1 > 0 else None
                    for i in range(n_p1):
                        xt, hT = nxt
                        for k in range(8):
                            MM(Ap[:, 0:288], hT[:, k, :], W1a[:, k, :], k == 0, k == 7, [hT, W1a], ["A"])
                        if i + 1 < n_p1:
                            nxt = front(xb[(i + 1) * 128:(i + 2) * 128, :])
                        ss, rs, kvn, kvnT = ssp.get(), rsp.get(), kvnp.get(), kvnTp.get()
                        ACT(jk[:], Ap[:, 0:256], AF.Square, ["A"], [jk, ss], accum_out=ss[:])
                        rstd_from_ss(rs, ss, 256)
                        ACT(kvn[:], Ap[:, 0:256], AF.Copy, ["A", rs], [kvn], scale=rs[:])
                        for k in range(2):
                            TR(T2[:, k * 128:(k + 1) * 128], kvn[:, k * 128:(k + 1) * 128], [kvn], ["T2"])
                        CP("dve", kvnT[:], T2[:, 0:256].rearrange("p (k t) -> p k t", k=2), ["T2"], [kvnT])
                        if stop == 13:
                            break
                        for r_ in range(2):
                            KV = KVp.get()
                            for pp in range(2):
                                pr = 2 * r_ + pp
                                for k in range(2):
                                    MM(KV[:, pp * 128:(pp + 1) * 128], Wkn[:, k, pr * 128:(pr + 1) * 128], kvnT[:, k, :],
                                       k == 0, k == 1, [Wkn, kvnT], [KV])
                            for k in range(2):
                                MM(KV[:, 256:512], kvnT[:, k, :], Wkv[:, k, r_ * 256:(r_ + 1) * 256], k == 0, k == 1,
                                   [Wkv, kvnT], [KV])
                            import os as _os
                            _sk = _os.environ.get("KSKIP", "")
                            if "b" not in _sk:
                                CP("act", KnT[:, 2 * r_:2 * r_ + 2, i * 128:(i + 1) * 128],
                                   KV[:, 0:256].rearrange("p (a t) -> p a t", a=2), [KV], [("KnT", i, r_)])
                            if "c" not in _sk:
                                CP("dve", V[:, i, 4 * r_:4 * r_ + 4, 0:64],
                                   KV[:, 256:512].rearrange("p (h d) -> p h d", h=4), [KV, "Vinit"], [("V", i, r_)])
                        if stop == 14:
                            break
                        t_, kp, kp4 = tp.get(), kpp.get(), kp4p.get()
                        x1_, x2_ = Ap[:, 256:272], Ap[:, 272:288]
                        c_, s_ = cosb[:, i, :], sinb[:, i, :]
                        TT("dve", t_[:, 0, :], x1_, c_, ALU.mult, ["A", cosb], [t_])
                        TT("dve", t_[:, 1, :], x2_, s_, ALU.mult, ["A", sinb], [t_])
                        TT("dve", t_[:, 2, :], x2_, c_, ALU.mult, ["A", cosb], [t_])
                        TT("dve", t_[:, 3, :], x1_, s_, ALU.mult, ["A", sinb], [t_])
                        TT("dve", kp[:, 0:16], t_[:, 0, :], t_[:, 1, :], ALU.subtract, [t_], [kp])
                        TT("dve", kp[:, 16:32], t_[:, 2, :], t_[:, 3, :], ALU.add, [t_], [kp])
                        CP("dve", kp4[:], kp[:].unsqueeze(1).to_broadcast([128, 4, 32]), [kp], [kp4])
                        TR(T2[:, 256:384], kp4[:].rearrange("p a d -> p (a d)"), [kp4], ["T2"])
                        CP("act", KpeT[:, i * 128:(i + 1) * 128], T2[:, 256:384], ["T2"], [("KpeT", i)])
                    S.barrier()

                if "p2" in phases:
                    with ExitStack() as p2:
                        Wq = sb("Wq", [128, 8, 384], BF16, p2)
                        Wqn = sb("Wqn", [128, 3, 512], BF16, p2)
                        Wqr = sb("Wqr", [128, 3, 256], BF16, p2)
                        gq = sb("gq", [128, 3], F32, p2)
                        maskf = sb("maskf", [128, 4, 128], F32, p2)
                        S.dma("sp", gq[:], g_q_lat.rearrange("(k p) -> p k", p=128), writes=[gq], slow=True)
                        S.dma("sp", maskf[:], masks[:, :, :], writes=[maskf])
                        scale = 96 ** -0.5
                        with ExitStack() as tmp:
                            load_w(tmp, Wq, w_in, C_QL[0], C_QL[1], 8, gmix)
                            qs = sb("qs", [128, 3, 8, 96], F32, tmp)
                            for k in range(3):
                                S.dma("sp", qs[:, k], w_qb[k * 128:(k + 1) * 128, :].rearrange("p (h d) -> p h d", h=8),
                                      writes=[qs])
                            for k in range(3):
                                TS("dve", Wqn[:, k, :].rearrange("p (h d) -> p h d", h=8), qs[:, k, :, 0:64], gq[:, k:k + 1], None,
                                   ALU.mult, None, [qs, gq], [Wqn])
                                TS("pool", Wqr[:, k, :].rearrange("p (h d) -> p h d", h=8), qs[:, k, :, 64:96], gq[:, k:k + 1], None,
                                   ALU.mult, None, [qs, gq], [Wqr])
                            S.barrier()
                        T = pst("p2T", [128, 1024], BF16, p2)
                        QL = pst("p2QL", [128, 512], F32, p2)
                        Qp = pst("p2Q", [128, 1024], F32, p2)
                        Sps = Pool(nc, p2, "p2S", [128, 512], F32, 2, psum=True)
                        S.excl.update(id(t) for t in Sps.t)
                        Ops = Pool(nc, p2, "p2O", [128, 512], F32, 2, psum=True)
                        S.excl.update(id(t) for t in Ops.t)
                        front = Front(p2, T, "T", nb=1)
                        jk = sb("p2jk", [128, 384], BF16, p2)
                        ssp = Pool(nc, p2, "p2ss", [128, 1], F32, 2)
                        rsp = Pool(nc, p2, "p2rs", [128, 1], F32, 2)
                        qlnp = Pool(nc, p2, "qln", [128, 384], BF16, 1)
                        qlnTp = Pool(nc, p2, "qlnT", [128, 3, 128], BF16, 1)
                        qzp = Pool(nc, p2, "qz", [128, 16, 128], BF16, 1)
                        for t in qzp.t:
                            S.op("pool", lambda e: e.memset(t[:], 0.0), [], [t])
                        qrp = Pool(nc, p2, "qr", [128, 8, 32], BF16, 2)
                        t4p = Pool(nc, p2, "p2t", [128, 4, 8, 16], F32, 1)
                        qTp = Pool(nc, p2, "qT", [128, 16, 128], BF16, 2)
                        PTp = Pool(nc, p2, "PT", [128, 512], BF16, 3)
                        PTf = Pool(nc, p2, "PTf", [128, 512], F32, 1)
                        recp = Pool(nc, p2, "rec", [128, 1], F32, 4)
                        yap = Pool(nc, p2, "ya", [128, 512], BF16, 2)
                        yaTp = Pool(nc, p2, "yaTt", [128, 512], BF16, 2)
                        yafp = Pool(nc, p2, "yaf", [128, 512 if dbg else 2], F32, 1)
                        def qpath(j):
                            xt, hT = front(xo[j * 128:(j + 1) * 128, :])
                            for k in range(8):
                                MM(QL[:, 0:384], hT[:, k, :], Wq[:, k, :], k == 0, k == 7, [hT, Wq], ["QL"])
                            ss, rs, qln, qlnT = ssp.get(), rsp.get(), qlnp.get(), qlnTp.get()
                            ACT(jk[:], QL[:, 0:384], AF.Square, ["QL"], [jk, ss], accum_out=ss[:])
                            rstd_from_ss(rs, ss, 384)
                            ACT(qln[:], QL[:, 0:384], AF.Copy, ["QL", rs], [qln], scale=rs[:])
                            for k in range(3):
                                TR(T[:, k * 128:(k + 1) * 128], qln[:, k * 128:(k + 1) * 128], [qln], ["T"])
                            CP("dve", qlnT[:], T[:, 0:384].rearrange("p (k t) -> p k t", k=3), ["T"], [qlnT])
                            for k in range(3):
                                MM(Qp[:, 0:512], qlnT[:, k, :], Wqn[:, k, :], k == 0, k == 2, [qlnT, Wqn], [("Q", 0)])
                            for k in range(3):
                                MM(Qp[:, 512:768], qlnT[:, k, :], Wqr[:, k, :], k == 0, k == 2, [qlnT, Wqr], [("Q", 1)])
                            qz, qr, t4, qT = qzp.get(), qrp.get(), t4p.get(), qTp.get()
                            qzf = qz[:].rearrange("p m d -> p (m d)")
                            for e_ in range(2):
                                ACT(qzf[:, 0:1024].rearrange("p (a x) -> p a x", x=256)[:, :, e_ * 192:e_ * 192 + 64],
                                    Qp[:, 0:512].rearrange("p (a e d) -> p a e d", e=2, d=64)[:, :, e_, :],
                                    AF.Copy, [("Q", 0)], [qz], scale=scale)
                            Qr3 = Qp[:, 512:768].rearrange("p (h d) -> p h d", h=8)
                            X1, X2 = Qr3[:, :, 0:16], Qr3[:, :, 16:32]
                            Cq = cosq[:, j, :].unsqueeze(1).to_broadcast([128, 8, 16])
                            Sq = sinq[:, j, :].unsqueeze(1).to_broadcast([128, 8, 16])
                            TT("dve", t4[:, 0], X1, Cq, ALU.mult, [("Q", 1), cosq], [t4])
                            TT("dve", t4[:, 1], X2, Sq, ALU.mult, [("Q", 1), sinq], [t4])
                            TT("dve", t4[:, 2], X2, Cq, ALU.mult, [("Q", 1), cosq], [t4])
                            TT("dve", t4[:, 3], X1, Sq, ALU.mult, [("Q", 1), sinq], [t4])
                            TT("dve", qr[:, :, 0:16], t4[:, 0], t4[:, 1], ALU.subtract, [t4], [qr])
                            TT("dve", qr[:, :, 16:32], t4[:, 2], t4[:, 3], ALU.add, [t4], [qr])
                            for b_ in range(4):
                                CP("pool", qzf[:, 1024:2048].rearrange("p (a y) -> p a y", y=512)[:, :, b_ * 160:b_ * 160 + 32],
                                   qr[:].rearrange("p (a b) d -> p a b d", b=4)[:, :, b_, :], [qr], [qz])
                            for half in range(2):
                                for m in range(8):
                                    TR(T[:, m * 128:(m + 1) * 128], qz[:, half * 8 + m, :], [qz], ["T"])
                                CP("dve" if half == 0 else "act", qT[:, half * 8:half * 8 + 8, :],
                                   T[:, 0:1024].rearrange("p (m t) -> p m t", m=8), ["T"], [qT])
                            return qT

                        nxt_q = qpath(0) if n_own > 0 else None
                        for j in range(n_own):
                            qT = nxt_q
                            ya, yaf = yap.get(), yafp.get()
                            steps = [(h, g) for h in range(8) for g in range(j + 1)]

                            def emit_S(h, g):
                                Sg = Sps.get()
                                for t in range(4):
                                    kt = 4 * g + t
                                    MM(Sg[:, t * 128:(t + 1) * 128], KnT[:, h // 2, kt * 128:(kt + 1) * 128], qT[:, h, :],
                                       True, False, [qT], [Sg])
                                    MM(Sg[:, t * 128:(t + 1) * 128], KpeT[:, kt * 128:(kt + 1) * 128], qT[:, 8 + h, :],
                                       False, True, [qT], [Sg])
                                return Sg

                            Sg_next = emit_S(*steps[0])
                            Op = None
                            for si, (h, g) in enumerate(steps):
                                if si == len(steps) // 2 and j + 1 < n_own:
                                    nxt_q = qpath(j + 1)
                                Sg = Sg_next
                                if si + 1 < len(steps):
                                    Sg_next = emit_S(*steps[si + 1])
                                if g == 0:
                                    Op = Ops.get()
                                okey = Op
                                PT = PTp.get()
                                if g == j:
                                    pf = PTf.get()
                                    ACT(pf[:], Sg[:], AF.Exp, [Sg], [pf])
                                    TT("dve", PT[:], pf[:], maskf[:].rearrange("p a q -> p (a q)"), ALU.mult, [pf, maskf], [PT])
                                else:
                                    ACT(PT[:], Sg[:], AF.Exp, [Sg], [PT])
                                for t in range(4):
                                    kt = 4 * g + t
                                    MM(Op[:, 0:65], PT[:, t * 128:(t + 1) * 128], V[:, kt, h, 0:65],
                                       g == 0 and t == 0, g == j and t == 3, [PT], [okey])
                                if g == j:
                                    rec = recp.get()
                                    S.op("dve", lambda e: e.reciprocal(out=rec[:], in_=Op[:, 64:65]), [okey], [rec])
                                    TS("dve", ya[:, h * 64:(h + 1) * 64], Op[:, 0:64], rec[:, 0:1], None, ALU.mult, None,
                                       [okey, rec], [ya])
                                    if dbg:
                                        TS("dve", yaf[:, h * 64:(h + 1) * 64], Op[:, 0:64], rec[:, 0:1], None, ALU.mult, None,
                                           [okey, rec], [yaf])
                            if dbg:
                                S.dma("sp", dbg_out["d_ya"][j * 128:(j + 1) * 128, :], yaf[:], reads=[yaf], writes=[("dya", j)])
                            for k in range(4):
                                TR(T[:, k * 128:(k + 1) * 128], ya[:, k * 128:(k + 1) * 128], [ya], ["T"])
                            yaTt = yaTp.get()
                            CP("dve", yaTt[:], T[:, 0:512], ["T"], [yaTt])
                            S.dma("sp", yaTs[j * 128:(j + 1) * 128, :], yaTt[:], reads=[yaTt], writes=[("yaTs", j)])
                        S.barrier()

        if "p3a" in phases:
            with ExitStack() as p3:
                Wgq = sb("Wgq", [128, 8, 256], BF16, p3)
                Wgb = sb("Wgb", [128, 8, 2560], BF16, p3)
                WA = sb("WA", [128, 4, 1024], BF16, p3)
                WB = sb("WB", [128, 4, 1024], BF16, p3)
                WO = sb("WO", [128, 8, 1024], BF16, p3)
                ggla = sb("ggla", [128, 512], F32, p3)
                cix = sb("cix", [128, 32], U32, p3)
                S.dma("sp", ggla[:], g_gla.unsqueeze(0).to_broadcast([128, 512]), writes=[ggla])
                S.dma("sp", cix[:], cidx[:, :], writes=[cix])
                with ExitStack() as tmp:
                    load_w(tmp, Wgq, w_in, C_GQ[0], C_GQ[1], 8, gmix)
                    load_w(tmp, Wgb, w_in, C_GO_BR[0], C_GO_BR[1], 8, gmix)
                    load_w(tmp, WA, w_ba, 0, 1024, 4)
                    load_w(tmp, WB, w_bb, 0, 1024, 4)
                    load_w(tmp, WO, w_out, 0, 1024, 8)
                    S.barrier()
                T = pst("p3T", [128, 1024], BF16, p3)
                G1 = pst("p3G1", [128, 512], F32, p3)
                GO = pst("p3GO", [128, 1024], F32, p3)
                BR = pst("p3BR", [128, 2048], F32, p3)
                front = Front(p3, T, "T")
                gqzp = Pool(nc, p3, "gqz", [128, 4, 2, 128], BF16, 2)
                for t in gqzp.t:
                    S.op("pool", lambda e: e.memset(t[:], 0.0), [], [t])
                stg = Pool(nc, p3, "stg", [128, 256], F32, 4)
                stb = Pool(nc, p3, "stb", [128, 256], BF16, 4)
                sgp = Pool(nc, p3, "sg", [128, 512], F32, 2)
                gatp = Pool(nc, p3, "gat", [128, 2048], F32, 2)
                jk = sb("p3jk", [128, 128], BF16, p3)
                ssp = Pool(nc, p3, "p3ss", [128, 4], F32, 2)
                rsp = Pool(nc, p3, "p3rs", [128, 4], F32, 2)
                ybp = Pool(nc, p3, "yb", [128, 512], BF16, 2)
                ybfp = Pool(nc, p3, "ybf", [128, 512], F32, 2)
                ybTp = Pool(nc, p3, "ybT", [128, 4, 128], BF16, 2)
                m1p = Pool(nc, p3, "m1", [128, 1024], F32, 2)
                mbp = Pool(nc, p3, "mb", [128, 1024], BF16, 2)
                mTp = Pool(nc, p3, "mT", [128, 8, 128], BF16, 2)
                x1p = Pool(nc, p3, "x1", [128, 1024], F32, 2)
                yaTp3 = Pool(nc, p3, "yaT3", [128, 4, 128], BF16, 2)
                nxt = front(xo[0:128, :]) if n_own > 0 else None
                for j in range(n_own):
                    xt, hT = nxt
                    yaT = yaTp3.get()
                    S.dma("sp", yaT[:].rearrange("p k t -> p (k t)"), yaTs[j * 128:(j + 1) * 128, :], writes=[yaT])
                    for p in range(2):
                        for k in range(8):
                            MM(G1[:, p * 128:(p + 1) * 128], Wgq[:, k, p * 128:(p + 1) * 128], hT[:, k, :], k == 0, k == 7,
                               [hT, Wgq], ["G1"])
                    for k in range(8):
                        MM(GO[:, 0:512], hT[:, k, :], Wgb[:, k, 0:512], k == 0, k == 7, [hT, Wgb], ["GO0"])
                    for n_ in range(4):
                        for k in range(8):
                            MM(BR[:, n_ * 512:(n_ + 1) * 512], hT[:, k, :], Wgb[:, k, 512 + n_ * 512:1024 + n_ * 512],
                               k == 0, k == 7, [hT, Wgb], [("BR", n_)])
                    if j + 1 < n_own:
                        nxt = front(xo[(j + 1) * 128:(j + 2) * 128, :])
                    gqz = gqzp.get()
                    for h in range(4):
                        for c in range(2):
                            rows = slice((h % 2) * 64, (h % 2) * 64 + 64)
                            TS("dve", gqz[rows, h, c, c * 64:(c + 1) * 64],
                               G1[rows, (h // 2) * 128 + c * 64:(h // 2) * 128 + (c + 1) * 64], 0.125, None, ALU.mult, None,
                               ["G1"], [gqz])
                    sbs = []
                    for c in range(2):
                        sg_, sb_ = stg.get(), stb.get()
                        S.dma("pool", sg_[:], states[:, :], reads=[cix], writes=[sg_],
                              indirect=bass.IndirectOffsetOnAxis(ap=cix[:, 2 * j + c:2 * j + c + 1], axis=0))
                        CP("dve", sb_[:], sg_[:], [sg_], [sb_])
                        sbs.append(sb_)
                    for h in range(4):
                        for c in range(2):
                            MM(GO[:, 512 + h * 128:512 + (h + 1) * 128], gqz[:, h, c, :],
                               sbs[c][:, (h // 2) * 128:(h // 2 + 1) * 128], c == 0, c == 1, [gqz, sbs[c]], ["GO1"])
                    sg, gat, ss, rs = sgp.get(), gatp.get(), ssp.get(), rsp.get()
                    ACT(sg[:], GO[:, 0:512], AF.Silu, ["GO0"], [sg])
                    TT("dve", sg[:], sg[:], ggla[:], ALU.mult, [sg, ggla], [sg])
                    for h in range(4):
                        ACT(jk[:], GO[:, 512 + h * 128:512 + (h + 1) * 128], AF.Square, ["GO1"], [jk, ss], accum_out=ss[:, h:h + 1])
                    rstd_from_ss(rs, ss, 128)
                    yb, ybf, ybT = ybp.get(), ybfp.get(), ybTp.get()
                    for h in range(4):
                        STT(yb[:, h * 128:(h + 1) * 128], GO[:, 512 + h * 128:512 + (h + 1) * 128], rs[:, h:h + 1],
                            sg[:, h * 128:(h + 1) * 128], ALU.mult, ALU.mult, ["GO1", rs, sg], [yb])
                        if dbg:
                            STT(ybf[:, h * 128:(h + 1) * 128], GO[:, 512 + h * 128:512 + (h + 1) * 128], rs[:, h:h + 1],
                                sg[:, h * 128:(h + 1) * 128], ALU.mult, ALU.mult, ["GO1", rs, sg], [ybf])
                    if dbg:
                        S.dma("sp", dbg_out["d_yb"][j * 128:(j + 1) * 128, :], ybf[:], reads=[ybf], writes=[("dyb", j)])
                    for k in range(4):
                        TR(T[:, k * 128:(k + 1) * 128], yb[:, k * 128:(k + 1) * 128], [yb], ["T"])
                    CP("dve", ybT[:], T[:, 0:512].rearrange("p (k t) -> p k t", k=4), ["T"], [ybT])
                    for n_ in range(4):
                        ACT(gat[:, n_ * 512:(n_ + 1) * 512], BR[:, n_ * 512:(n_ + 1) * 512], AF.Sigmoid, [("BR", n_)], [gat])
                    for n_ in range(2):
                        for k in range(4):
                            MM(BR[:, n_ * 512:(n_ + 1) * 512], yaT[:, k, :], WA[:, k, n_ * 512:(n_ + 1) * 512],
                               k == 0, k == 3, [WA, gat, yaT], [("BR", n_)])
                    for n_ in range(2):
                        for k in range(4):
                            MM(BR[:, 1024 + n_ * 512:1024 + (n_ + 1) * 512], ybT[:, k, :], WB[:, k, n_ * 512:(n_ + 1) * 512],
                               k == 0, k == 3, [WB, ybT, gat], [("BR", 2 + n_)])
                    m1, mb, mT, x1 = m1p.get(), mbp.get(), mTp.get(), x1p.get()
                    TT("dve", m1[:], gat[:, 0:1024], BR[:, 0:1024], ALU.mult, [gat, ("BR", 0), ("BR", 1)], [m1])
                    TT("dve", gat[:, 1024:2048], gat[:, 1024:2048], BR[:, 1024:2048], ALU.mult, [gat, ("BR", 2), ("BR", 3)], [gat])
                    TT("pool", mb[:], m1[:], gat[:, 1024:2048], ALU.add, [m1, gat], [mb])
                    for k in range(8):
                        TR(T[:, k * 128:(k + 1) * 128], mb[:, k * 128:(k + 1) * 128], [mb], ["T"])
                    CP("dve", mT[:], T[:, 0:1024].rearrange("p (k t) -> p k t", k=8), ["T"], [mT])
                    for n_ in range(2):
                        for k in range(8):
                            MM(GO[:, n_ * 512:(n_ + 1) * 512], mT[:, k, :], WO[:, k, n_ * 512:(n_ + 1) * 512], k == 0, k == 7,
                               [mT, WO], ["GO0", "GO1"])
                    TT("dve", x1[:], xt[:], GO[:, :], ALU.add, [xt, "GO0", "GO1"], [x1])
                    S.dma("sp", x1s[j * 128:(j + 1) * 128, :], x1[:], reads=[x1], writes=[("x1s", j)])
                    if dbg:
                        S.dma("sp", dbg_out["d_x1"][j * 128:(j + 1) * 128, :], x1[:], reads=[x1], writes=[("dx1", j)])
                S.barrier()

        if "p3b" in phases:
            with ExitStack() as p4:
                gffn = sb("gffn", [128, 1024], F32, p4)
                gfin = sb("gfin", [128, 1024], F32, p4)
                S.dma("sp", gffn[:], g_ffn.unsqueeze(0).to_broadcast([128, 1024]), writes=[gffn])
                S.dma("sp", gfin[:], g_final.unsqueeze(0).to_broadcast([128, 1024]), writes=[gfin])
                eiP = Pool(nc, p4, "q_ei", [128, 128], U32, NT_OWN)
                gtP = Pool(nc, p4, "q_gt", [128, 128], F32, NT_OWN)
                hbP = Pool(nc, p4, "q_hb", [128, 1024], BF16, NT_OWN)
                T = pst("p4T", [128, 1024], BF16, p4)
                QP = pst("p4QP", [128, 1024], F32, p4)
                SC = pst("p4SC", [128, 1024], F32, p4)
                Yp = pst("p4Y", [128, 1024], F32, p4)
                with ExitStack() as pr:
                    WP = sb("WP", [128, 8, 2048], BF16, pr)
                    skT = sb("skT", [128, 16, 128], BF16, pr)
                    with ExitStack() as tmp:
                        load_w(tmp, WP, w_pq, 0, 2048, 8)
                        sks = sb("sks", [128, 16, 128], F32, tmp)
                        skb = sb("skb", [128, 16, 128], BF16, tmp)
                        S.dma("sp", sks[:], subk.rearrange("m n d -> n m d"), writes=[sks])
                        CP("dve", skb[:], sks[:], [sks], [skb])
                        for half in range(2):
                            for m in range(8):
                                TR(T[:, m * 128:(m + 1) * 128], skb[:, half * 8 + m, :], [skb], ["T"])
                            CP("dve", skT[:, half * 8:half * 8 + 8, :], T[:, 0:1024].rearrange("p (m t) -> p m t", m=8), ["T"], [skT])
                        S.barrier()
                    x1p = Pool(nc, pr, "q_x1", [128, 1024], F32, 2)
                    hTp = Pool(nc, pr, "q_hT", [128, 8, 128], BF16, 2)
                    ssp = Pool(nc, pr, "q_ss", [128, 1], F32, 4)
                    rsp = Pool(nc, pr, "q_rs", [128, 1], F32, 4)
                    jk = sb("q_jk", [128, 1024], BF16, pr)
                    qpTp = Pool(nc, pr, "qpT", [128, 16, 128], BF16, 2)
                    wkp = Pool(nc, pr, "q_wk", [128, 256], F32, 4)
                    m16p = Pool(nc, pr, "q_m16", [128, 16, 16], F32, 2)
                    i16p = Pool(nc, pr, "q_i16", [128, 16, 16], U32, 2)
                    candp = Pool(nc, pr, "q_cand", [128, 8, 256], F32, 2)
                    c16p = Pool(nc, pr, "q_c16", [128, 8, 16], F32, 2)
                    p16p = Pool(nc, pr, "q_p16", [128, 8, 16], U32, 2)
                    abp = Pool(nc, pr, "q_ab", [128, 2, 128], U32, 2)
                    ohp = Pool(nc, pr, "q_oh", [128, 128, 16], F32, 2)
                    selp = Pool(nc, pr, "q_sel", [128, 2, 128], F32, 2)
                    efp = Pool(nc, pr, "q_ef", [128, 128], F32, 2)
                    zp = Pool(nc, pr, "q_z", [128, 8], F32, 2)
                    def front_gen(j):
                        x1, hb, hT = x1p.get(), hbP.t[j], hTp.get()
                        ss, rs = ssp.get(), rsp.get()
                        S.dma("sp", x1[:], x1s[j * 128:(j + 1) * 128, :], reads=[("x1s", j)], writes=[x1])
                        ACT(jk[:], x1[:], AF.Square, [x1], [jk, ss], accum_out=ss[:])
                        rstd_from_ss(rs, ss, 1024)
                        yield
                        STT(hb[:], x1[:], rs[:, 0:1], gffn[:], ALU.mult, ALU.mult, [x1, rs, gffn], [hb])
                        yield
                        for k in range(8):
                            TR(T[:, k * 128:(k + 1) * 128], hb[:, k * 128:(k + 1) * 128], [hb], ["T"])
                        CP("dve", hT[:], T[:, 0:1024].rearrange("p (k t) -> p k t", k=8), ["T"], [hT])
                        yield
                        qpT = qpTp.get()
                        for half in range(2):
                            for m in range(8):
                                hp = half * 8 + m
                                for k in range(8):
                                    MM(QP[:, m * 128:(m + 1) * 128], WP[:, k, hp * 128:(hp + 1) * 128], hT[:, k, :], k == 0, k == 7,
                                       [WP, hT], ["QP"])
                            CP("act", qpT[:, half * 8:half * 8 + 8, :], QP[:, :].rearrange("p (m t) -> p m t", m=8), ["QP"], [qpT])
                            yield
                        m16, i16 = m16p.get(), i16p.get()
                        for half in range(2):
                            for m in range(8):
                                hp = half * 8 + m
                                MM(SC[:, m * 128:(m + 1) * 128], qpT[:, hp, :], skT[:, hp, :], True, True, [qpT, skT], [("SC", m // 4)])
                            for m0 in range(0, 8, 2):
                                ms = (m0, m0 + 1)
                                hps = [half * 8 + m for m in ms]
                                wks = [wkp.get() for _ in ms]
                                scs = [SC[:, m * 128:(m + 1) * 128] for m in ms]
                                kys = [("SC", m // 4) for m in ms]
                                for i_ in range(2):
                                    S.op("dve", lambda e: e.max(out=m16[:, hps[i_], 0:8], in_=scs[i_]), [kys[i_]], [(id(m16), hps[i_])])
                                for i_ in range(2):
                                    S.op("dve", lambda e: e.max_index(out=i16[:, hps[i_], 0:8], in_max=m16[:, hps[i_], 0:8], in_values=scs[i_]),
                                         [kys[i_], (id(m16), hps[i_])], [(id(i16), hps[i_])])
                                for i_ in range(2):
                                    S.op("dve", lambda e: e.match_replace(out=wks[i_][:, 0:128], in_to_replace=m16[:, hps[i_], 0:8],
                                                                          in_values=scs[i_], imm_value=-1e30),
                                         [kys[i_], (id(m16), hps[i_])], [wks[i_]])
                                for i_ in range(2):
                                    S.op("dve", lambda e: e.max(out=m16[:, hps[i_], 8:16], in_=wks[i_][:, 0:128]), [wks[i_]], [(id(m16), hps[i_], 1)])
                                for i_ in range(2):
                                    S.op("dve", lambda e: e.max_index(out=i16[:, hps[i_], 8:16], in_max=m16[:, hps[i_], 8:16],
                                                                      in_values=wks[i_][:, 0:128]),
                                         [wks[i_], (id(m16), hps[i_], 1)], [(id(i16), hps[i_], 1)])
                            yield
                        cand, c16, p16 = candp.get(), c16p.get(), p16p.get()
                        m4 = m16[:].rearrange("p (h s) k -> p h s k", s=2)
                        m16k = [(id(m16), hp_) for hp_ in range(16)] + [(id(m16), hp_, 1) for hp_ in range(16)]
                        i16k = [(id(i16), hp_) for hp_ in range(16)] + [(id(i16), hp_, 1) for hp_ in range(16)]
                        TT("dve", cand[:].rearrange("p h (a b) -> p h a b", a=16),
                           m4[:, :, 0, :].unsqueeze(3).to_broadcast([128, 8, 16, 16]),
                           m4[:, :, 1, :].unsqueeze(2).to_broadcast([128, 8, 16, 16]), ALU.add, m16k, [cand])
                        yield
                        for h in range(8):
                            wk = wkp.get()
                            cd = cand[:, h, :]
                            S.op("dve", lambda e: e.max(out=c16[:, h, 0:8], in_=cd), [cand], [c16])
                            S.op("dve", lambda e: e.max_index(out=p16[:, h, 0:8], in_max=c16[:, h, 0:8], in_values=cd),
                                 [cand, c16], [p16])
                            S.op("dve", lambda e: e.match_replace(out=wk[:], in_to_replace=c16[:, h, 0:8], in_values=cd,
                                                                  imm_value=-1e30), [cand, c16], [wk])
                            S.op("dve", lambda e: e.max(out=c16[:, h, 8:16], in_=wk[:]), [wk], [c16])
                            S.op("dve", lambda e: e.max_index(out=p16[:, h, 8:16], in_max=c16[:, h, 8:16], in_values=wk[:]),
                                 [wk, c16], [p16])
                            yield
                        ab, oh, sel, ef, ei = abp.get(), ohp.get(), selp.get(), efp.get(), eiP.t[j]
                        p16f = p16[:].rearrange("p h k -> p (h k)")
                        TS("dve", ab[:, 0, :], p16f, 4, None, ALU.logical_shift_right, None, [p16], [ab])
                        TS("dve", ab[:, 1, :], p16f, 15, None, ALU.bitwise_and, None, [p16], [ab])
                        i4 = i16[:].rearrange("p (h s) k -> p h s k", s=2)
                        for s_ in range(2):
                            TT("dve", oh[:], ab[:, s_, :].unsqueeze(2).to_broadcast([128, 128, 16]),
                               iota16[:].unsqueeze(1).to_broadcast([128, 128, 16]), ALU.is_equal, [ab, iota16], [oh])
                            yield
                            TT("dve", oh[:].rearrange("p (h k) a -> p h k a", h=8), oh[:].rearrange("p (h k) a -> p h k a", h=8),
                               i4[:, :, s_, :].unsqueeze(2).to_broadcast([128, 8, 16, 16]), ALU.mult, [oh] + i16k, [oh])
                            yield
                            S.op("dve", lambda e: e.tensor_reduce(out=sel[:, s_, :], in_=oh[:], axis=AX.X, op=ALU.add),
                                 [oh], [sel])
                            yield
                        STT(ef[:], sel[:, 0, :], 128.0, sel[:, 1, :], ALU.mult, ALU.add, [sel], [ef])
                        CP("dve", ei[:], ef[:], [ef], [ei])
                        gt, z = gtP.t[j], zp.get()
                        TT("dve", gt[:].rearrange("p (h k) -> p h k", h=8), c16[:], c16[:, :, 0:1].to_broadcast([128, 8, 16]),
                           ALU.subtract, [c16], [gt])
                        ACT(gt[:], gt[:], AF.Exp, [gt], [gt])
                        S.op("dve", lambda e: e.tensor_reduce(out=z[:], in_=gt[:].rearrange("p (h k) -> p h k", h=8), axis=AX.X,
                                                              op=ALU.add), [gt], [z])
                        S.op("dve", lambda e: e.reciprocal(out=z[:], in_=z[:]), [z], [z])
                        TT("dve", gt[:].rearrange("p (h k) -> p h k", h=8), gt[:].rearrange("p (h k) -> p h k", h=8),
                           z[:].unsqueeze(2).to_broadcast([128, 8, 16]), ALU.mult, [gt, z], [gt])
                        return None


                    gens = [front_gen(j) for j in range(n_own)]
                    active, nx = [], 0
                    while nx < n_own or active:
                        while len(active) < 2 and nx < n_own:
                            active.append(gens[nx])
                            nx += 1
                        for g_ in list(active):
                            try:
                                next(g_)
                            except StopIteration:
                                active.remove(g_)
                    S.barrier()
                S.wait_bg()
                with ExitStack() as pe:
                    x1p = Pool(nc, pe, "e_x1", [128, 1024], F32, 2)
                    ssp = Pool(nc, pe, "e_ss", [128, 1], F32, 2)
                    rsp = Pool(nc, pe, "e_rs", [128, 1], F32, 2)
                    jk = sb("e_jk", [128, 1024], BF16, pe)
                    ap_ = Pool(nc, pe, "q_a", [128, 128], F32, 2)
                    cp_ = Pool(nc, pe, "q_c", [128, 128], F32, 2)
                    gbp = Pool(nc, pe, "q_gb", [128, 2048], BF16, 24)
                    Dp4 = Pool(nc, pe, "q_D", [128, 8, 128], BF16, 4)
                    jk2 = sb("q_jk2", [128, 1024], BF16, pe)
                    yp = Pool(nc, pe, "q_y", [128, 1024], F32, 2)
                    for j in range(n_own):
                        x1, hb, ei, gt = x1p.get(), hbP.t[j], eiP.t[j], gtP.t[j]
                        S.dma("sp", x1[:], x1s[j * 128:(j + 1) * 128, :], reads=[("x1s", j)], writes=[x1])
                        a_, c_ = ap_.get(), cp_.get()
                        for grp in range(16):
                            gbs = []
                            for kk in range(8):
                                e_ = grp * 8 + kk
                                gb = gbp.get()
                                gbs.append(gb)
                                S.dma("pool", gb[:], uvb[:, :], reads=[ei], writes=[gb],
                                      indirect=bass.IndirectOffsetOnAxis(ap=ei[:, e_:e_ + 1], axis=0))
                                S.op("dve", lambda e: e.scalar_tensor_tensor(out=jk2[:], in0=gb[:, 0:1024], scalar=1.0, in1=hb[:],
                                                                             op0=ALU.mult, op1=ALU.mult,
                                                                             accum_out=a_[:, e_:e_ + 1]),
                                     [gb, hb], [jk2, ("a", j, grp)])
                            g0 = grp * 8
                            ACT(c_[:, g0:g0 + 8], a_[:, g0:g0 + 8], AF.Gelu, [("a", j, grp)], [("c", j, grp)])
                            TT("dve", c_[:, g0:g0 + 8], c_[:, g0:g0 + 8], gt[:, g0:g0 + 8], ALU.mult, [("c", j, grp), gt], [("c", j, grp)])
                            Dk = Dp4.get()
                            for kk in range(8):
                                ACT(Dk[:, kk, :], identb[:], AF.Copy, [identb, ("c", j, grp)], [Dk], scale=c_[:, g0 + kk:g0 + kk + 1])
                            for kk in range(8):
                                e_ = grp * 8 + kk
                                for n_ in range(2):
                                    MM(Yp[:, n_ * 512:(n_ + 1) * 512], Dk[:, kk, :], gbs[kk][:, 1024 + n_ * 512:1024 + (n_ + 1) * 512],
                                       e_ == 0, e_ == 127, [Dk, gbs[kk]], [("Y", n_)])
                        y = yp.get()
                        TT("dve", y[:], x1[:], Yp[:, :], ALU.add, [x1, ("Y", 0), ("Y", 1)], [y])
                        ss2, rs2 = ssp.get(), rsp.get()
                        ACT(jk[:], y[:], AF.Square, [y], [jk, ss2], accum_out=ss2[:])
                        rstd_from_ss(rs2, ss2, 1024)
                        STT(y[:], y[:], rs2[:, 0:1], gfin[:], ALU.mult, ALU.mult, [y, rs2, gfin], [y])
                        S.dma("sp", out[j * 128:(j + 1) * 128, :], y[:], reads=[y], writes=[("out", j)])
                    S.barrier()
        S.barrier()
        print("program: insts=%d waits=%d sems=%d" % (S.n_inst, S.n_wait, S.nsem))
    return nc


def _consts():
    t = np.arange(128)
    same = (t[:, None] // 64) == (t[None, :] // 64)
    rem = (same & (t[:, None] > t[None, :])).astype(np.float32) / 16.0
    cind = np.stack([(t < 64), (t >= 64)], axis=1).astype(np.float32)
    iota = np.tile(np.arange(16, dtype=np.float32)[None, :], (128, 1))
    invf = (10000.0 ** (-np.arange(16, dtype=np.float32) / np.float32(16))).astype(np.float32)
    invf = np.tile(invf[None, :], (128, 1)).astype(np.float32)
    return {"c_ident": np.eye(128, dtype=np.float32), "c_rem": rem, "c_cind": cind, "c_cind16": cind / 16.0,
            "c_iota": iota, "c_invf": invf}


def _masks(r):
    m = np.zeros((128, 4, 128), np.float32)
    k = np.arange(128)
    for t in range(4):
        if t < r:
            m[:, t, :] = 1.0
        elif t == r:
            m[:, t, :] = ((k[:, None] // 64) <= (k[None, :] // 64)).astype(np.float32)
    return m


def make_in_maps(inputs):
    x = np.asarray(inputs["x"], np.float32)
    pos = np.asarray(inputs["positions"], np.int32)
    cst = _consts()
    shared = {
        "g_mix": inputs["g_mix"].reshape(1024), "w_in": inputs["w_in"].reshape(1024, 4272),
        "g_q_lat": inputs["g_q_lat"].reshape(384), "w_qb": inputs["w_qb"].reshape(384, 768),
        "g_kv_lat": inputs["g_kv_lat"].reshape(256), "w_kvb": inputs["w_kvb"].reshape(256, 1024),
        "w_a2": inputs["w_a2"].reshape(16, 256), "b_a2": inputs["b_a2"].reshape(256),
        "g_gla": inputs["g_gla"].reshape(512), "w_branch_a": inputs["w_branch_a"].reshape(512, 1024),
        "w_branch_b": inputs["w_branch_b"].reshape(512, 1024), "w_out": inputs["w_out"].reshape(1024, 1024),
        "g_ffn": inputs["g_ffn"].reshape(1024), "w_peer_q": inputs["w_peer_q"].reshape(1024, 2048),
        "peer_sub_keys": inputs["peer_sub_keys"].reshape(16, 128, 128),
        "peer_u": inputs["peer_u"].reshape(16384, 1024), "peer_v": inputs["peer_v"].reshape(16384, 1024),
        "g_final": inputs["g_final"].reshape(1024),
    }
    shared = {k: np.ascontiguousarray(np.asarray(v, np.float32)) for k, v in shared.items()}
    shared.update(cst)
    maps = []
    for c in range(8):
        b, r = c // 4, c % 4
        xt = x[b].reshape(64, 128, 1024)
        own = np.arange(16) * 4 + r
        pt = pos[b].reshape(64, 128)
        p = np.arange(128)
        cidx = np.zeros((128, 32), np.uint32)
        for j in range(16):
            for cc in range(2):
                cidx[:, 2 * j + cc] = (8 * j + 2 * r + cc) * 128 + p
        m = dict(shared)
        m["xb"] = np.ascontiguousarray(x[b])
        m["xo"] = np.ascontiguousarray(xt[own].reshape(2048, 1024))
        m["posb"] = np.ascontiguousarray(pt.T)
        m["poso"] = np.ascontiguousarray(pt[own].T)
        m["cidx"] = cidx
        m["masks"] = _masks(r)
        maps.append(m)
    return maps


def assemble(results):
    outp = np.zeros((2, 64, 128, 1024), np.float32)
    for c in range(8):
        b, r = c // 4, c % 4
        own = np.arange(16) * 4 + r
        outp[b, own] = np.asarray(results[c]["out"], np.float32).reshape(16, 128, 1024)
    return outp.reshape(2, 8192, 1024)


def kernel(**inputs):
    nc = build_program()
    maps = make_in_maps(inputs)
    res = run_bass_kernel_spmd(nc, maps, core_ids=list(range(8)))
    return assemble(res.results)
```
